# Optimizing a Trainium2 kernel written in Bass

```python
import math
import jax, jax.numpy as jnp
from jax import lax
import numpy as np

D_MODEL = 1024
BATCH = 16
SEQ = 2048
DEPTH = 1

PLE_DIM = 256
D_MIX = D_MODEL
M_HEADS = 4
M_DV = (D_MIX // 2) // M_HEADS
M_DK = M_DV // 2
M_CHUNK = 64
CONV_K = 3
A_HEADS = 4
A_DV = (D_MIX // 2) // A_HEADS
A_DK = A_DV // 2
Q_BLOCK = 128
D_FF = 2816
EPS = 1e-6

M_QK = M_HEADS * M_DK
M_V = M_HEADS * M_DV
M_GATES = 4 * M_HEADS
A_QK = A_HEADS * 2 * A_DK
A_V = A_HEADS * A_DV
SPLITS = (M_QK, M_QK, M_V, M_V, M_GATES, A_QK, A_QK, A_V)
D_IN = sum(SPLITS)

kernel_name = "hybrid_mlstm_diffattn_macaron_encoder"


def rmsnorm(x, g):
    xf = x.astype(jnp.float32)
    y = xf * lax.rsqrt(jnp.mean(xf * xf, axis=-1, keepdims=True) + EPS) * g.astype(jnp.float32)
    return y.astype(x.dtype)


def head_rms(x):
    return x * lax.rsqrt(jnp.mean(x * x, axis=-1, keepdims=True) + EPS)


def swiglu(x, w_in, w_out):
    gu = x @ w_in
    g, u = jnp.split(gu, 2, axis=-1)
    return (jax.nn.silu(g) * u) @ w_out


def centred_dwconv(x, w):
    c = x.shape[-1]
    pad = (CONV_K - 1) // 2
    return lax.conv_general_dilated(
        x, w.astype(x.dtype)[:, None, :], window_strides=(1,), padding=((pad, pad),),
        dimension_numbers=('NWC', 'WIO', 'NWC'), feature_group_count=c)


def mlstm_chunkwise(q, k, v, log_i, f_pre):
    B, H, S, DK = q.shape
    DV = v.shape[-1]
    L = M_CHUNK
    NC = S // L
    log_f = jax.nn.log_sigmoid(f_pre)

    def chunks(t):
        return jnp.moveaxis(t.reshape((B, H, NC, L) + t.shape[3:]), 2, 0)

    xs = (chunks(q), chunks(k), chunks(v), chunks(log_i), chunks(log_f))
    tril = jnp.tril(jnp.ones((L, L), dtype=bool))

    def step(carry, inp):
        C, n, m = carry
        qc, kc, vc, li, lf = inp
        b = jnp.cumsum(lf, axis=-1)
        d = jnp.where(tril, b[..., :, None] - b[..., None, :] + li[..., None, :], -jnp.inf)
        inter = b + m[..., None]
        mj = jnp.maximum(inter, jnp.max(d, axis=-1))
        s = jnp.einsum('bhld,bhsd->bhls', qc, kc) * jnp.exp(d - mj[..., None])
        w_inter = jnp.exp(inter - mj)
        num = (w_inter[..., None] * jnp.einsum('bhld,bhdv->bhlv', qc, C)
               + jnp.einsum('bhls,bhsv->bhlv', s, vc))
        den = w_inter * jnp.einsum('bhld,bhd->bhl', qc, n) + jnp.sum(s, axis=-1)
        hc = num / jnp.maximum(jnp.abs(den), jnp.exp(-mj))[..., None]
        bl = b[..., -1]
        g = bl[..., None] - b + li
        m_new = jnp.maximum(bl + m, jnp.max(g, axis=-1))
        decay = jnp.exp(bl + m - m_new)
        wk = jnp.exp(g - m_new[..., None])
        C_new = decay[..., None, None] * C + jnp.einsum('bhsd,bhsv->bhdv', kc * wk[..., None], vc)
        n_new = decay[..., None] * n + jnp.einsum('bhs,bhsd->bhd', wk, kc)
        return (C_new, n_new, m_new), hc

    init = (jnp.zeros((B, H, DK, DV), jnp.float32),
            jnp.zeros((B, H, DK), jnp.float32),
            jnp.zeros((B, H), jnp.float32))
    _, hs = lax.scan(step, init, xs)
    return jnp.moveaxis(hs, 0, 2).reshape(B, H, S, DV)


def mlstm_group(q, k, v, o_pre, gates, conv_w, b_gate, norm_g):
    B, S, _ = q.shape
    dt = q.dtype
    qk = jax.nn.silu(centred_dwconv(jnp.concatenate([q, k], axis=-1), conv_w))
    q, k = jnp.split(qk, 2, axis=-1)
    qh = q.astype(jnp.float32).reshape(B, S, M_HEADS, M_DK).transpose(0, 2, 1, 3)
    kh = (k.astype(jnp.float32) * (M_DK ** -0.5)).reshape(B, S, M_HEADS, M_DK).transpose(0, 2, 1, 3)
    vh = v.astype(jnp.float32).reshape(B, S, M_HEADS, M_DV).transpose(0, 2, 1, 3)
    gt = (gates.astype(jnp.float32) + b_gate.astype(jnp.float32)).reshape(B, S, 4, M_HEADS)
    gt = gt.transpose(2, 0, 3, 1)
    i_fw, f_fw, i_bw, f_bw = gt[0], gt[1], gt[2], gt[3]
    h_fw = mlstm_chunkwise(qh, kh, vh, i_fw, f_fw)
    flip = lambda t: jnp.flip(t, axis=2)
    h_bw = flip(mlstm_chunkwise(flip(qh), flip(kh), flip(vh), jnp.flip(i_bw, -1), jnp.flip(f_bw, -1)))
    h = head_rms(h_fw + h_bw)
    h = h.transpose(0, 2, 1, 3).reshape(B, S, M_V) * norm_g.astype(jnp.float32)
    return (jax.nn.sigmoid(o_pre.astype(jnp.float32)) * h).astype(dt)


def diff_attn_group(q, k, v, lam_q1, lam_k1, lam_q2, lam_k2, norm_g, lam_init):
    B, S, _ = q.shape
    dt = v.dtype
    qh = q.reshape(B, S, A_HEADS, 2, A_DK).transpose(0, 2, 3, 1, 4)
    kh = k.reshape(B, S, A_HEADS, 2, A_DK).transpose(0, 2, 3, 1, 4)
    vh = v.reshape(B, S, A_HEADS, A_DV).transpose(0, 2, 1, 3)
    scale = A_DK ** -0.5
    slopes = jnp.asarray(np.array([2.0 ** (-8.0 * (h + 1) / A_HEADS) for h in range(A_HEADS)],
                                  dtype=np.float32))
    f32 = jnp.float32
    lam = (jnp.exp(jnp.sum(lam_q1.astype(f32) * lam_k1.astype(f32)))
           - jnp.exp(jnp.sum(lam_q2.astype(f32) * lam_k2.astype(f32))) + lam_init)
    NB = S // Q_BLOCK
    qb = jnp.moveaxis(qh.reshape(B, A_HEADS, 2, NB, Q_BLOCK, A_DK), 3, 0)
    kpos = jnp.arange(S, dtype=jnp.int32)
    starts = jnp.arange(NB, dtype=jnp.int32) * Q_BLOCK

    def block(args):
        qblk, start = args
        s = jnp.einsum('bhcqd,bhckd->bhcqk', qblk, kh).astype(f32) * scale
        qpos = start + jnp.arange(Q_BLOCK, dtype=jnp.int32)
        dist = jnp.abs(qpos[:, None] - kpos[None, :]).astype(f32)
        s = s - slopes[None, :, None, None, None] * dist
        pr = jax.nn.softmax(s, axis=-1)
        a = pr[:, :, 0] - lam * pr[:, :, 1]
        return jnp.einsum('bhqk,bhkd->bhqd', a.astype(dt), vh)

    o = lax.map(block, (qb, starts))
    o = jnp.moveaxis(o, 0, 2).reshape(B, A_HEADS, S, A_DV)
    o = head_rms(o.astype(f32)) * (1.0 - lam_init)
    o = o.transpose(0, 2, 1, 3).reshape(B, S, A_V) * norm_g.astype(f32)
    return o.astype(dt)


def setup_inputs(seed: int = 0) -> dict:
    key = jax.random.key(seed)
    ks = jax.random.split(key, 24)
    f32 = jnp.float32
    nrm = lambda k, shape, s: jax.random.normal(k, shape, f32) * s
    gain = lambda k, shape: 1.0 + 0.05 * jax.random.normal(k, shape, f32)
    gate_noise = 0.1 * jax.random.normal(ks[7], (DEPTH, M_GATES), f32)
    f_bias = jnp.linspace(3.0, 6.0, M_HEADS, dtype=f32)
    zero_h = jnp.zeros((M_HEADS,), f32)
    b_mgate = gate_noise + jnp.concatenate([zero_h, f_bias, zero_h, f_bias])[None, :]
    return {
        "x": jax.random.normal(ks[0], (BATCH, SEQ, D_MODEL), f32),
        "p": jax.random.normal(ks[1], (DEPTH, BATCH, SEQ, PLE_DIM), f32),
        "g_ffn1": gain(ks[2], (DEPTH, D_MODEL)),
        "w_ffn1_in": nrm(ks[3], (DEPTH, D_MODEL, 2 * D_FF), D_MODEL ** -0.5),
        "w_ffn1_out": nrm(ks[4], (DEPTH, D_FF, D_MODEL), D_FF ** -0.5),
        "g_mix": gain(ks[5], (DEPTH, D_MODEL)),
        "w_in": nrm(ks[6], (DEPTH, D_MODEL, D_IN), D_MODEL ** -0.5),
        "b_mgate": b_mgate,
        "conv_w": nrm(ks[8], (DEPTH, CONV_K, 2 * M_QK), CONV_K ** -0.5),
        "g_mnorm": gain(ks[9], (DEPTH, M_V)),
        "lam_q1": nrm(ks[10], (DEPTH, A_DK), 0.1),
        "lam_k1": nrm(ks[11], (DEPTH, A_DK), 0.1),
        "lam_q2": nrm(ks[12], (DEPTH, A_DK), 0.1),
        "lam_k2": nrm(ks[13], (DEPTH, A_DK), 0.1),
        "g_anorm": gain(ks[14], (DEPTH, A_V)),
        "w_out": nrm(ks[15], (DEPTH, D_MIX, D_MODEL), D_MIX ** -0.5),
        "g_ffn2": gain(ks[16], (DEPTH, D_MODEL)),
        "w_ffn2_in": nrm(ks[17], (DEPTH, D_MODEL, 2 * D_FF), D_MODEL ** -0.5),
        "w_ffn2_out": nrm(ks[18], (DEPTH, D_FF, D_MODEL), D_FF ** -0.5),
        "g_ple": gain(ks[19], (DEPTH, D_MODEL)),
        "w_ple_gate": nrm(ks[20], (DEPTH, D_MODEL, D_MODEL), D_MODEL ** -0.5),
        "w_ple_proj": nrm(ks[21], (DEPTH, PLE_DIM, D_MODEL), PLE_DIM ** -0.5),
        "g_final": gain(ks[22], (D_MODEL,)),
    }


def reference(x, p, g_ffn1, w_ffn1_in, w_ffn1_out, g_mix, w_in, b_mgate, conv_w, g_mnorm,
              lam_q1, lam_k1, lam_q2, lam_k2, g_anorm, w_out, g_ffn2, w_ffn2_in, w_ffn2_out,
              g_ple, w_ple_gate, w_ple_proj, g_final):
    h = x
    bounds = np.cumsum(np.array(SPLITS))[:-1].tolist()
    for i in range(DEPTH):
        lam_init = 0.8 - 0.6 * math.exp(-0.3 * i)
        h = h + 0.5 * swiglu(rmsnorm(h, g_ffn1[i]), w_ffn1_in[i], w_ffn1_out[i])
        u = rmsnorm(h, g_mix[i])
        z = u @ w_in[i]
        mq, mk, mv, mo, mg, aq, ak, av = jnp.split(z, bounds, axis=-1)
        y_m = mlstm_group(mq, mk, mv, mo, mg, conv_w[i], b_mgate[i], g_mnorm[i])
        y_a = diff_attn_group(aq, ak, av, lam_q1[i], lam_k1[i], lam_q2[i], lam_k2[i],
                              g_anorm[i], lam_init)
        h = h + jnp.concatenate([y_m, y_a], axis=-1) @ w_out[i]
        h = h + 0.5 * swiglu(rmsnorm(h, g_ffn2[i]), w_ffn2_in[i], w_ffn2_out[i])
        gate = jax.nn.sigmoid((rmsnorm(h, g_ple[i]) @ w_ple_gate[i]).astype(jnp.float32))
        h = h + (gate * (p[i] @ w_ple_proj[i]).astype(jnp.float32)).astype(h.dtype)
    return rmsnorm(h, g_final)
```

```python
import contextlib
import numpy as np
import concourse.bass as bass
import concourse.mybir as mybir
from concourse.bass_utils import run_bass_kernel_spmd

F32 = mybir.dt.float32
BF16 = mybir.dt.bfloat16
AF = mybir.ActivationFunctionType
ALU = mybir.AluOpType

NCORES = 8
D = 1024
DFF = 2816
NF = DFF // 128
SEQ = 2048
TOK = 2 * SEQ
CH = 512
NCH = SEQ // CH
EPS = 1e-6
LAM_INIT = 0.8 - 0.6


class Res:
    __slots__ = ("name", "last_w", "readers", "dsem")

    def __init__(self, name=""):
        self.name = name
        self.last_w = None
        self.readers = []
        self.dsem = None


class Op:
    __slots__ = ("eng", "idx", "emit", "waits", "know", "dma", "dsem", "dval", "flag")


class Sched:
    ENG = ("pe", "act", "dve", "pool", "sp")

    def __init__(self, nc):
        self.nc = nc
        self.ops = {e: [] for e in self.ENG}
        self.cidx = {e: 0 for e in self.ENG}
        self.know = {e: {} for e in self.ENG}
        self.ndsem = 0
        self.dcnt = {}

    def new_dsem(self):
        i = self.ndsem
        self.ndsem += 1
        self.dcnt[i] = 0
        return i

    def op(self, eng, emit, reads=(), writes=(), dma=False, dsem=None):
        o = Op()
        o.eng = eng
        o.emit = emit
        o.dma = dma
        o.flag = False
        if dma:
            if dsem is None:
                r0 = writes[0]
                if r0.dsem is None:
                    r0.dsem = self.new_dsem()
                dsem = r0.dsem
            o.dsem = dsem
            self.dcnt[dsem] += 16
            o.dval = self.dcnt[dsem]
            o.idx = -1
        else:
            o.idx = self.cidx[eng]
            self.cidx[eng] += 1
        deps = []
        for r in reads:
            if r.last_w is not None:
                deps.append(r.last_w)
        for r in writes:
            if r.last_w is not None:
                deps.append(r.last_w)
            deps.extend(r.readers)
        know = self.know[eng]
        waits = {}
        eff = []
        for d in deps:
            if d is o:
                continue
            if d.dma:
                if dma and d.dsem == o.dsem:
                    continue
                key, val = ("d", d.dsem), d.dval
            else:
                if d.eng == eng and not dma:
                    if eng == "pe":
                        continue
                key, val = ("e", d.eng), d.idx + 1
            eff.append(d)
            if know.get(key, 0) >= val:
                continue
            if waits.get(key, (0, None))[0] < val:
                waits[key] = (val, d)
        for d in eff:
            for k, v in d.know.items():
                if know.get(k, 0) < v:
                    know[k] = v
        for key, (val, d) in waits.items():
            if know.get(key, 0) < val:
                know[key] = val
            d.flag = True
        o.waits = [(key, d) for key, (val, d) in sorted(waits.items(), key=lambda kv: str(kv[0]))]
        o.know = dict(know)
        self.ops[eng].append(o)
        for r in reads:
            r.readers.append(o)
        for r in writes:
            r.last_w = o
            r.readers = []
        return o

    def barrier(self, extra_ops=()):
        toks = []
        for o in extra_ops:
            r = Res()
            r.last_w = o
            toks.append(r)
        for e in ("pe", "act", "dve"):
            last = None
            for o in reversed(self.ops[e]):
                if not o.dma:
                    last = o
                    break
            if last is not None:
                r = Res()
                r.last_w = last
                toks.append(r)
        for e in self.ENG:
            self.op(e, lambda eh: eh.nop(), reads=toks)

    def stats(self):
        return {e: (len(self.ops[e]), sum(len(o.waits) for o in self.ops[e])) for e in self.ENG}

    def emit_all(self, final_ops=()):
        nc = self.nc
        cnt = {}
        for e in self.ENG:
            c = 0
            for o in self.ops[e]:
                if o.dma:
                    continue
                if o.flag:
                    c += 1
                cnt[(e, o.idx)] = c
        with contextlib.ExitStack() as st:
            esem = {e: st.enter_context(nc.semaphore("s_" + e)) for e in self.ENG}
            dsem = [st.enter_context(nc.semaphore("d%d" % i)) for i in range(self.ndsem)]
            block = st.enter_context(nc.Block())

            def emit_wait(eh, key, d):
                if key[0] == "d":
                    eh.wait_ge(dsem[key[1]], d.dval)
                else:
                    eh.wait_ge(esem[key[1]], cnt[(d.eng, d.idx)])

            def run(engname, eh):
                for o in self.ops[engname]:
                    for key, d in o.waits:
                        emit_wait(eh, key, d)
                    ins = o.emit(eh)
                    if o.dma:
                        ins.then_inc(dsem[o.dsem], 16)
                    elif o.flag:
                        ins.then_inc(esem[engname], 1)

            @block.tensor
            def _(e):
                run("pe", e)

            @block.scalar
            def _(e):
                run("act", e)

            @block.vector
            def _(e):
                run("dve", e)

            @block.gpsimd
            def _(e):
                run("pool", e)
                done = {}
                for o in final_ops:
                    done[o.dsem] = max(done.get(o.dsem, 0), o.dval)
                for k, v in done.items():
                    e.wait_ge(dsem[k], v)

            @block.sync
            def _(e):
                run("sp", e)


C_G = 0
C_CQ = C_G + 40
C_CK = C_CQ + 6
C_BI = C_CK + 6
C_BF = C_BI + 1
C_LAM = C_BF + 1
C_GM = C_LAM + 256
C_GA = C_GM + 512
C_ID = C_GA + 512
C_MU = C_ID + 128
C_ML = C_MU + 128
C_SEL = C_ML + 128
NCST = C_SEL + 512


def pack_consts(inp):
    c = np.zeros((128, NCST), np.float32)
    for gi, k in enumerate(("g_ffn1", "g_mix", "g_ffn2", "g_ple")):
        c[:, C_G + gi * 8:C_G + gi * 8 + 8] = inp[k][0].reshape(8, 128).T
    c[:, C_G + 32:C_G + 40] = inp["g_final"].reshape(8, 128).T
    cw = inp["conv_w"][0]
    for g in range(2):
        for k in range(3):
            c[:, C_CQ + g * 3 + k] = cw[k, g * 128:(g + 1) * 128]
            c[:, C_CK + g * 3 + k] = cw[k, 256 + g * 128:256 + (g + 1) * 128]
    b = inp["b_mgate"][0]
    c[0:4, C_BI] = b[0:4]
    c[32:36, C_BI] = b[8:12]
    c[0:4, C_BF] = b[4:8]
    c[32:36, C_BF] = b[12:16]
    for i, k in enumerate(("lam_q1", "lam_k1", "lam_q2", "lam_k2")):
        c[:, C_LAM + i * 64:C_LAM + (i + 1) * 64] = inp[k][0][None, :]
    c[:, C_GM:C_GM + 512] = inp["g_mnorm"][0][None, :]
    c[:, C_GA:C_GA + 512] = inp["g_anorm"][0][None, :]
    c[:, C_ID:C_ID + 128] = np.eye(128, dtype=np.float32)
    i = np.arange(128)
    c[:, C_MU:C_MU + 128] = (i[:, None] <= i[None, :]).astype(np.float32)
    c[:, C_ML:C_ML + 128] = (i[:, None] >= i[None, :]).astype(np.float32)
    for h in range(4):
        c[h, C_SEL + h * 128:C_SEL + (h + 1) * 128] = 1.0
        c[32 + h, C_SEL + h * 128:C_SEL + (h + 1) * 128] = 1.0
    return c


WIN_SLABS = {}
for _g in range(2):
    WIN_SLABS["mq%d" % _g] = _g * 128
    WIN_SLABS["mk%d" % _g] = 256 + _g * 128
for _h in range(4):
    WIN_SLABS["mv%d" % _h] = 512 + _h * 128
    WIN_SLABS["mo%d" % _h] = 1024 + _h * 128
    WIN_SLABS["aq%d" % _h] = 1552 + _h * 128
    WIN_SLABS["ak%d" % _h] = 2064 + _h * 128
    WIN_SLABS["av%d" % _h] = 2576 + _h * 128
WIN_NAMES = list(WIN_SLABS.keys()) + ["gate"]
WIN_IDX = {n: i for i, n in enumerate(WIN_NAMES)}


def build_nc(stage="full"):
    nc = bass.Bass("TRN2", target_bir_lowering=False)
    S = Sched(nc)

    def dram(name, shape, dt, kind):
        return nc.dram_tensor(name, shape, dt, kind=kind).ap()

    xT = dram("xT", [D, TOK], F32, "ExternalInput")
    pT = dram("pT", [256, TOK], F32, "ExternalInput")
    w1i = dram("w1i", [D, 2 * DFF], F32, "ExternalInput")
    w1o = dram("w1o", [DFF, D], F32, "ExternalInput")
    w2i = dram("w2i", [D, 2 * DFF], F32, "ExternalInput")
    w2o = dram("w2o", [DFF, D], F32, "ExternalInput")
    win = dram("win", [D, 3088], F32, "ExternalInput")
    wout = dram("wout", [D, D], F32, "ExternalInput")
    wpg = dram("wpg", [D, D], F32, "ExternalInput")
    wpp = dram("wpp", [256, D], F32, "ExternalInput")
    cstd = dram("cst", [128, NCST], F32, "ExternalInput")
    zer = dram("zer", [128, 1024], F32, "ExternalInput")
    outT = dram("outT", [D, TOK], F32, "ExternalOutput")
    s_f1i = dram("s_f1i", [2 * NF, 128, 8, 128], BF16, "Internal")
    s_f1o = dram("s_f1o", [8, 128, NF, 128], BF16, "Internal")
    s_f2i = dram("s_f2i", [2 * NF, 128, 8, 128], BF16, "Internal")
    s_f2o = dram("s_f2o", [8, 128, NF, 128], BF16, "Internal")
    s_win = dram("s_win", [len(WIN_NAMES), 128, 8, 128], BF16, "Internal")
    s_wout = dram("s_wout", [2, 8, 128, 4, 128], BF16, "Internal")
    s_wpg = dram("s_wpg", [8, 128, 8, 128], BF16, "Internal")
    s_gs = dram("s_gs", [4, 128, 3968], BF16, "Internal")

    with contextlib.ExitStack() as st:
        def sb(name, shape, dt):
            return st.enter_context(nc.sbuf_tensor(name, shape, dt))

        H = sb("H", [128, 8, SEQ], F32)
        U = sb("U", [128, 8, SEQ], BF16)
        Y = sb("Y", [128, 4, SEQ], BF16)
        NA = 5
        RA = [sb("ra%d" % i, [128, 8, 128], BF16) for i in range(NA)]
        NB_ = 2
        ARENA_F32 = 16384
        ARENA = sb("arena", [128, ARENA_F32], F32)
        arena_pos = {"o": 0}

        def carve(shape, dt):
            n = 1
            for d_ in shape:
                n *= d_
            words = (n * (2 if dt == BF16 else 4) + 3) // 4
            o = arena_pos["o"]
            arena_pos["o"] = o + words
            assert arena_pos["o"] <= ARENA_F32, arena_pos
            ap = ARENA[:, o:o + words]
            if dt == BF16:
                ap = ap.bitcast(BF16)[:, 0:n]
            if len(shape) == 2:
                ap = ap.rearrange("p (a b) -> p a b", b=shape[1])
            elif len(shape) == 3:
                ap = ap.rearrange("p (a b c) -> p a b c", b=shape[1], c=shape[2])
            return ap

        arena_pos["o"] = 0
        RB = [carve([NF, 128], BF16) for i in range(NB_)]
        HID = carve([NF, CH], BF16)
        SG = [carve([CH], F32) for i in range(2)]
        GSTMP = [carve([3968], BF16) for i in range(2)]
        R_gstmp = [Res(), Res()]
        SQ = [sb("sq%d" % i, [128, CH], BF16) for i in range(2)]
        RT = sb("rt", [128, CH], F32)
        RSTD = sb("rstd", [128, CH], F32)
        CST = sb("cstt", [128, NCST], F32)
        PW = sb("pw", [128, 2, D], BF16)
        PTB = sb("ptb", [128, 2, CH], BF16)
        onesb = sb("onesb", [128, 128], BF16)
        identb = sb("identb", [128, 128], BF16)
        PSALL = st.enter_context(nc.psum_tensor("psall", [128, 8, 512], F32))
        PS = [PSALL[:, i, :] for i in range(8)]

        PSR = [Res("ps%d" % i) for i in range(8)]
        Hres = [[Res() for dc in range(8)] for j in range(NCH)]
        Ures = [[Res() for dc in range(8)] for j in range(NCH)]
        Yres = [[Res() for k in range(4)] for j in range(NCH)]
        RAres = [Res() for _ in range(NA)]
        RBres = [Res() for _ in range(NB_)]
        HIDres = [Res() for _ in range(NF)]
        SGres = [Res(), Res()]
        SQres = [Res(), Res()]
        R_rt, R_rstd, R_cst, R_pw, R_ptb, R_ones, R_identb = [Res() for _ in range(7)]
        OUTres = [Res() for _ in range(NCH)]
        ring = {"a": 0, "b": 0}

        S.op("sp", lambda e: e.dma_start(out=CST[:], in_=cstd[:, :]), writes=[R_cst], dma=True)
        S.op("dve", lambda e: e.memset(onesb[:], 1.0), writes=[R_ones])
        S.op("dve", lambda e: e.tensor_copy(out=identb[:], in_=CST[:, C_ID:C_ID + 128]), reads=[R_cst], writes=[R_identb])

        gate_tok = {"r": []}

        def conv_group(dst, src_w, cols_list, kch, res):
            for i, c0 in cols_list:
                S.op("pool", lambda e, i=i, c0=c0: e.dma_start(
                    out=dst[i], in_=src_w[:, c0:c0 + 128].rearrange("(c p) n -> p c n", p=128)),
                    reads=gate_tok["r"], writes=[res], dma=True)

        def ffn_in_groups(dst, src_w, pre=None):
            groups = []
            for g0 in range(0, NF, 6):
                r = Res() if pre is None else pre[g0 // 6]
                lst = []
                for f in range(g0, min(NF, g0 + 6)):
                    lst.append((2 * f, f * 128))
                    lst.append((2 * f + 1, DFF + f * 128))
                conv_group(dst, src_w, lst, 8, r)
                groups.append(r)
            return [groups[f // 6] for f in range(NF)]

        R_f1i = ffn_in_groups(s_f1i, w1i)
        R_f1o = Res()
        conv_group(s_f1o, w1o, [(dm, dm * 128) for dm in range(8)], NF, R_f1o)
        R_win = {}
        R_gz, R_wa, R_wm, R_gate, R_wout = Res(), Res(), Res(), Res(), Res()
        for n in WIN_NAMES:
            R_win[n] = R_wa if n[0] == "a" else R_wm
        R_win["gate"] = R_gate
        R_f2i = [Res() for _ in range(4)]
        R_f2i = [R_f2i[f // 6] for f in range(NF)]
        R_f2o, R_wpg = Res(), Res()

        def conv_stage2():
            S.op("pool", lambda e: e.dma_start(out=s_win[WIN_IDX["gate"]], in_=zer[:, :].rearrange("p (c n) -> p c n", n=128)),
                 writes=[R_gz], dma=True)
            conv_group(s_win, win, [(WIN_IDX[n], WIN_SLABS[n]) for n in WIN_NAMES if n[0] == "a"], 8, R_wa)
            conv_group(s_win, win, [(WIN_IDX[n], WIN_SLABS[n]) for n in WIN_NAMES if n[0] == "m"], 8, R_wm)
            for (dcol, zcol) in ((0, 1536), (32, 1544), (64, 1540), (96, 1548)):
                S.op("pool", lambda e, dcol=dcol, zcol=zcol: e.dma_start(
                    out=s_win[WIN_IDX["gate"]][:, :, dcol:dcol + 4],
                    in_=win[:, zcol:zcol + 4].rearrange("(c p) n -> p c n", p=128)),
                    reads=[R_gz], writes=[R_gate], dma=True)
            for grp in range(2):
                for dm in range(8):
                    r0 = 0 if grp == 0 else 512
                    S.op("pool", lambda e, grp=grp, dm=dm, r0=r0: e.dma_start(
                        out=s_wout[grp, dm], in_=wout[r0:r0 + 512, dm * 128:(dm + 1) * 128].rearrange("(c p) n -> p c n", p=128)),
                        writes=[R_wout], dma=True)

        def conv_stage3():
            ffn_in_groups(s_f2i, w2i, [R_f2i[0], R_f2i[6], R_f2i[12], R_f2i[18]])
            conv_group(s_f2o, w2o, [(dm, dm * 128) for dm in range(8)], NF, R_f2o)
            conv_group(s_wpg, wpg, [(dm, dm * 128) for dm in range(8)], 8, R_wpg)
            S.op("pool", lambda e: e.dma_start(out=PW[:], in_=wpp[:, :].rearrange("(c p) n -> p c n", p=128)),
                 writes=[R_pw], dma=True)


        def loadA(src_ap, src_res, kch=8):
            slot = ring["a"] % NA
            ring["a"] += 1
            S.op("sp", lambda e: e.dma_start(out=RA[slot][:, 0:kch, :], in_=src_ap), reads=[src_res], writes=[RAres[slot]], dma=True)
            return RA[slot], RAres[slot]

        def loadB(src_ap, src_res):
            slot = ring["b"] % NB_
            ring["b"] += 1
            S.op("sp", lambda e: e.dma_start(out=RB[slot], in_=src_ap), reads=[src_res], writes=[RBres[slot]], dma=True)
            return RB[slot], RBres[slot]

        def mm(out, lhsT, rhs, start, stop, reads, writes, sgc=False):
            S.op("pe", lambda e: e.matmul(out, lhsT=lhsT, rhs=rhs, start=start, stop=stop, skip_group_check=sgc), reads=reads, writes=writes)

        def load_x(s, j):
            c0 = s * SEQ + j * CH
            S.op("sp", lambda e: e.dma_start(out=H[:, :, j * CH:(j + 1) * CH],
                                             in_=xT[:, c0:c0 + CH].rearrange("(c p) n -> p c n", p=128)),
                 writes=Hres[j], dma=True)

        def hap(j, dc):
            return H[:, dc, j * CH:(j + 1) * CH]

        def uap(j, dc):
            return U[:, dc, j * CH:(j + 1) * CH]

        def norm(j, gidx, dst_ap, dst_res):
            for dc in range(8):
                k = dc % 2
                S.op("act", lambda e, dc=dc, k=k: e.activation(out=SQ[k][:], in_=hap(j, dc), func=AF.Square),
                     reads=[Hres[j][dc]], writes=[SQres[k]])
                mm(PS[0], onesb[:], SQ[k][:], dc == 0, dc == 7, [SQres[k], R_ones], [PSR[0]])
            S.op("act", lambda e: e.activation(out=RT[:], in_=PS[0], func=AF.Sqrt, scale=1.0 / D, bias=EPS),
                 reads=[PSR[0]], writes=[R_rt])
            S.op("dve", lambda e: e.reciprocal(out=RSTD[:], in_=RT[:]), reads=[R_rt], writes=[R_rstd])
            for dc in range(8):
                S.op("dve", lambda e, dc=dc: e.scalar_tensor_tensor(
                    out=dst_ap(j, dc), in0=hap(j, dc), scalar=CST[:, C_G + gidx * 8 + dc:C_G + gidx * 8 + dc + 1],
                    in1=RSTD[:], op0=ALU.mult, op1=ALU.mult),
                    reads=[Hres[j][dc], R_rstd, R_cst], writes=[dst_res[j][dc]])
            return S.ops["dve"][-1]

        def ffn(j, s_in, R_in, s_o, R_o):
            for f in range(NF):
                gt, gr = loadA(s_in[2 * f], R_in[f])
                ut, ur = loadA(s_in[2 * f + 1], R_in[f])
                pg, pu = 1 + f % 2, 3 + f % 2
                for dc in range(8):
                    mm(PS[pg], gt[:, dc, :], uap(j, dc), dc == 0, dc == 7, [gr, Ures[j][dc]], [PSR[pg]])
                for dc in range(8):
                    mm(PS[pu], ut[:, dc, :], uap(j, dc), dc == 0, dc == 7, [ur, Ures[j][dc]], [PSR[pu]])
                k = f % 2
                S.op("act", lambda e, pg=pg, k=k: e.activation(out=SG[k][:], in_=PS[pg], func=AF.Silu),
                     reads=[PSR[pg]], writes=[SGres[k]])
                S.op("dve", lambda e, pu=pu, k=k, f=f: e.tensor_tensor(out=HID[:, f, :], in0=PS[pu], in1=SG[k][:], op=ALU.mult),
                     reads=[PSR[pu], SGres[k]], writes=[HIDres[f]])
            for dm in range(8):
                ot, orr = loadB(s_o[dm], R_o)
                po = 5 + dm % 2
                for fc in range(NF):
                    mm(PS[po], ot[:, fc, :], HID[:, fc, :], fc == 0, fc == NF - 1, [orr, HIDres[fc]], [PSR[po]])
                S.op("dve", lambda e, dm=dm, po=po: e.scalar_tensor_tensor(
                    out=hap(j, dm), in0=PS[po], scalar=0.5, in1=hap(j, dm), op0=ALU.mult, op1=ALU.add),
                    reads=[PSR[po], Hres[j][dm]], writes=[Hres[j][dm]])

        def phase_a(s, j):
            norm(j, 0, uap, Ures)
            ffn(j, s_f1i, R_f1i, s_f1o, R_f1o)
            return norm(j, 1, uap, Ures)

        def wout_proj(j, grp):
            for dm in range(8):
                wt, wr = loadA(s_wout[grp, dm], R_wout, 4)
                po = 5 + dm % 2
                for kc in range(4):
                    mm(PS[po], wt[:, kc, :], Y[:, kc, j * CH:(j + 1) * CH], kc == 0, kc == 3, [wr, Yres[j][kc]], [PSR[po]])
                S.op("dve", lambda e, dm=dm, po=po: e.tensor_tensor(out=hap(j, dm), in0=PS[po], in1=hap(j, dm), op=ALU.add),
                     reads=[PSR[po], Hres[j][dm]], writes=[Hres[j][dm]])

        def phase_c(s, j):
            norm(j, 2, uap, Ures)
            ffn(j, s_f2i, R_f2i, s_f2o, R_f2o)
            norm(j, 3, uap, Ures)
            c0 = s * SEQ + j * CH
            S.op("pool", lambda e: e.dma_start(out=PTB[:], in_=pT[:, c0:c0 + CH].rearrange("(c p) n -> p c n", p=128)),
                 writes=[R_ptb], dma=True)
            for dm in range(8):
                gt, gr = loadA(s_wpg[dm], R_wpg)
                pg, pu = 1 + dm % 2, 3 + dm % 2
                k = dm % 2
                for dc in range(8):
                    mm(PS[pg], gt[:, dc, :], uap(j, dc), dc == 0, dc == 7, [gr, Ures[j][dc]], [PSR[pg]])
                for pc in range(2):
                    mm(PS[pu], PW[:, pc, dm * 128:(dm + 1) * 128], PTB[:, pc, :], pc == 0, pc == 1, [R_pw, R_ptb], [PSR[pu]])
                S.op("act", lambda e, pg=pg, k=k: e.activation(out=SG[k][:], in_=PS[pg], func=AF.Sigmoid),
                     reads=[PSR[pg]], writes=[SGres[k]])
                S.op("dve", lambda e, pu=pu, k=k: e.tensor_tensor(out=SG[k][:], in0=PS[pu], in1=SG[k][:], op=ALU.mult),
                     reads=[PSR[pu], SGres[k]], writes=[SGres[k]])
                S.op("dve", lambda e, dm=dm, k=k: e.tensor_tensor(out=hap(j, dm), in0=hap(j, dm), in1=SG[k][:], op=ALU.add),
                     reads=[SGres[k], Hres[j][dm]], writes=[Hres[j][dm]])
            norm(j, 4, hap, Hres)
            return store(s, j)

        def store(s, j):
            c0 = s * SEQ + j * CH
            return S.op("pool", lambda e: e.dma_start(out=outT[:, c0:c0 + CH].rearrange("(c p) n -> p c n", p=128),
                                                      in_=H[:, :, j * CH:(j + 1) * CH]),
                        reads=Hres[j], writes=[OUTres[j]], dma=True)

        LAMT = sb("lamt", [128, 4], F32)
        LTMP = sb("ltmp", [128, 128], F32)
        R_lam, R_ltmp = Res(), Res()
        AX = mybir.AxisListType.X
        S.op("dve", lambda e: e.tensor_tensor(out=LTMP[:, 0:64], in0=CST[:, C_LAM:C_LAM + 64], in1=CST[:, C_LAM + 64:C_LAM + 128], op=ALU.mult),
             reads=[R_cst], writes=[R_ltmp])
        S.op("dve", lambda e: e.tensor_tensor(out=LTMP[:, 64:128], in0=CST[:, C_LAM + 128:C_LAM + 192], in1=CST[:, C_LAM + 192:C_LAM + 256], op=ALU.mult),
             reads=[R_cst], writes=[R_ltmp])
        S.op("dve", lambda e: e.reduce_sum(out=LAMT[:, 2:3], in_=LTMP[:, 0:64], axis=AX), reads=[R_ltmp], writes=[R_lam])
        S.op("dve", lambda e: e.reduce_sum(out=LAMT[:, 3:4], in_=LTMP[:, 64:128], axis=AX), reads=[R_ltmp], writes=[R_lam])
        S.op("act", lambda e: e.activation(out=LAMT[:, 2:4], in_=LAMT[:, 2:4], func=AF.Exp), reads=[R_lam], writes=[R_lam])
        S.op("dve", lambda e: e.tensor_tensor(out=LAMT[:, 0:1], in0=LAMT[:, 2:3], in1=LAMT[:, 3:4], op=ALU.subtract), reads=[R_lam], writes=[R_lam])
        S.op("dve", lambda e: e.tensor_scalar(out=LAMT[:, 0:1], in0=LAMT[:, 0:1], scalar1=LAM_INIT, scalar2=None, op0=ALU.add), reads=[R_lam], writes=[R_lam])
        S.op("dve", lambda e: e.tensor_scalar(out=LAMT[:, 1:2], in0=LAMT[:, 0:1], scalar1=-1.0, scalar2=None, op0=ALU.mult), reads=[R_lam], writes=[R_lam])

        arena_pos["o"] = 0
        GW = 3968
        QTs = [carve([SEQ], BF16) for _ in range(2)]
        KTs = [carve([SEQ], BF16) for _ in range(2)]
        VAs = [carve([16, 129], BF16) for _ in range(2)]
        GSs = [carve([GW], BF16) for _ in range(2)]
        EX = [carve([2, 512], BF16) for _ in range(2)]
        EG = [carve([2, 512], BF16) for _ in range(4)]
        ACCS = carve([8, 129], F32)
        OT = carve([4, 128], F32)
        T1 = carve([128], F32)
        YA4 = carve([4, 128], BF16)
        SQJ = carve([128], F32)
        REC = carve([16], F32)
        RS4 = carve([8], F32)
        ZR = carve([129], BF16)
        R_qts = [[Res() for _ in range(NCH)] for _ in range(2)]
        R_kts = [[Res() for _ in range(NCH)] for _ in range(2)]
        R_vas = [[Res() for _ in range(8)] for _ in range(2)]
        R_gss = [[Res() for _ in range(8)] for _ in range(2)]
        R_ex, R_eg = [Res(), Res()], [Res(), Res(), Res(), Res()]
        R_accs = Res()
        R_ot = [Res() for _ in range(4)]
        R_t1, R_ya4, R_sqj, R_rec, R_rs4, R_zr = Res(), Res(), Res(), Res(), Res(), Res()
        PSB7 = PSALL[:, 7, :].bitcast(BF16)
        _rb7 = Res()
        R_b7 = [_rb7, _rb7]

        def acc(a):
            return PSALL[:, 4 + a // 3, (a % 3) * 129:(a % 3) * 129 + 129]

        R_sgs = [Res() for _ in range(4)]

        arena_dma_reads = []

        def gen_strips():
            iob = Y[:, :, :].rearrange("p a b -> p (a b)").bitcast(F32)[:, 0:GW]
            yall = [r_ for l_ in Yres for r_ in l_]
            S.op("pool", lambda e: e.iota(iob, [[1, GW]], base=-1920, channel_multiplier=-1, allow_small_or_imprecise_dtypes=True), writes=yall)
            S.op("act", lambda e: e.activation(out=iob, in_=iob, func=AF.Abs), reads=yall, writes=yall)
            for h in range(4):
                slope = 2.0 ** (-2.0 * (h + 1))
                k = h % 2
                S.op("act", lambda e, k=k, slope=slope: e.activation(out=GSTMP[k], in_=iob, func=AF.Exp, scale=-slope), reads=yall, writes=[R_gstmp[k]])
                arena_dma_reads.append(S.op("sp", lambda e, h=h, k=k: e.dma_start(out=s_gs[h], in_=GSTMP[k]), reads=[R_gstmp[k]], writes=[R_sgs[h]], dma=True))

        def attention(s, dbg=False):
            half = {"n": 0}

            def next_half():
                hf = half["n"] % 2
                half["n"] += 1
                return hf

            for st_ in range(2):
                S.op("dve", lambda e, st_=st_: e.memset(VAs[st_][:, :, 128:129], 1.0), writes=R_vas[st_])
            S.op("dve", lambda e: e.memset(ZR, 0.0), writes=[R_zr])

            def prep_units(h, st_):
                QT, KT, VA, GS = QTs[st_], KTs[st_], VAs[st_], GSs[st_]
                slope = 2.0 ** (-2.0 * (h + 1))
                units = []
                slabs = {}

                def strip_load():
                    S.op("sp", lambda e: e.dma_start(out=GS, in_=s_gs[h]), reads=[R_sgs[h]], writes=R_gss[st_], dma=True)
                units.append(strip_load)

                def proj_qk(nm, dst, dres, j, hf):
                    if nm not in slabs:
                        slabs[nm] = loadA(s_win[WIN_IDX[nm]], R_win[nm])
                    wt, wr = slabs[nm]
                    bh = next_half()
                    c0 = j * CH + hf * 256
                    for dc in range(8):
                        mm(PSALL[:, 7, bh * 256:(bh + 1) * 256], wt[:, dc, :], U[:, dc, c0:c0 + 256], dc == 0, dc == 7, [wr, Ures[j][dc]], [R_b7[bh]])
                    S.op("dve", lambda e: e.tensor_copy(out=dst[:, c0:c0 + 256], in_=PSALL[:, 7, bh * 256:(bh + 1) * 256]),
                         reads=[R_b7[bh]], writes=[dres[j]])
                for j in range(NCH):
                    for hf in range(2):
                        units.append(lambda j=j, hf=hf: proj_qk("aq%d" % h, QT, R_qts[st_], j, hf))
                for j in range(NCH):
                    for hf in range(2):
                        units.append(lambda j=j, hf=hf: proj_qk("ak%d" % h, KT, R_kts[st_], j, hf))

                def proj_v(t2):
                    nm = "av%d" % h
                    if nm not in slabs:
                        slabs[nm] = loadA(s_win[WIN_IDX[nm]], R_win[nm])
                    wt, wr = slabs[nm]
                    bh = next_half()
                    for tq in range(2):
                        tt = t2 * 2 + tq
                        for dc in range(8):
                            mm(PSALL[:, 7, bh * 256 + tq * 128:bh * 256 + (tq + 1) * 128], U[:, dc, tt * 128:(tt + 1) * 128], wt[:, dc, :], dc == 0, dc == 7,
                               [wr, Ures[tt // 4][dc]], [R_b7[bh]])
                    S.op("dve", lambda e: e.tensor_copy(out=VA[:, t2 * 2:t2 * 2 + 2, 0:128],
                                                        in_=PSALL[:, 7, bh * 256:(bh + 1) * 256].rearrange("p (a b) -> p a b", b=128)),
                         reads=[R_b7[bh]], writes=[R_vas[st_][t2]])
                for t2 in range(8):
                    units.append(lambda t2=t2: proj_v(t2))
                return units

            def epilogue_rounds(qc, h):
                rounds = []
                r0 = [lambda: S.op("dve", lambda e: e.reciprocal(out=REC[:, 0:8].unsqueeze(2), in_=ACCS[:, :, 128:129]), reads=[R_accs], writes=[R_rec]),
                      lambda: S.op("dve", lambda e: e.tensor_scalar(out=REC[:, 8:12], in0=REC[:, 4:8], scalar1=LAMT[:, 1:2], scalar2=None, op0=ALU.mult),
                                   reads=[R_rec, R_lam], writes=[R_rec])]
                rounds.append(r0)
                for qb in range(4):
                    rounds.append([
                        lambda qb=qb: S.op("dve", lambda e: e.tensor_scalar(out=T1, in0=ACCS[:, 4 + qb, 0:128], scalar1=REC[:, 8 + qb:9 + qb], scalar2=None, op0=ALU.mult),
                                           reads=[R_accs, R_rec], writes=[R_t1]),
                        lambda qb=qb: S.op("dve", lambda e: e.scalar_tensor_tensor(out=OT[:, qb, :], in0=ACCS[:, qb, 0:128], scalar=REC[:, qb:qb + 1], in1=T1,
                                                                                  op0=ALU.mult, op1=ALU.add),
                                           reads=[R_accs, R_rec, R_t1], writes=[R_ot[qb]]),
                        lambda qb=qb: S.op("act", lambda e: e.activation(out=SQJ, in_=OT[:, qb, :], func=AF.Square, accum_out=REC[:, 12 + qb:13 + qb]),
                                           reads=[R_ot[qb]], writes=[R_sqj, R_rs4])])
                rounds.append([])
                rounds.append([lambda: S.op("act", lambda e: e.activation(out=RS4[:, 0:4], in_=REC[:, 12:16], func=AF.Sqrt, scale=1.0 / 128, bias=EPS),
                                            reads=[R_rs4], writes=[R_rs4])])
                rounds.append([lambda: S.op("dve", lambda e: e.reciprocal(out=RS4[:, 4:8], in_=RS4[:, 0:4]), reads=[R_rs4], writes=[R_rs4]),
                               lambda: S.op("dve", lambda e: e.tensor_scalar(out=RS4[:, 4:8], in0=RS4[:, 4:8], scalar1=1.0 - LAM_INIT, scalar2=None, op0=ALU.mult),
                                            reads=[R_rs4], writes=[R_rs4])])
                rounds.append([(lambda qb=qb: S.op("dve", lambda e: e.scalar_tensor_tensor(out=YA4[:, qb, :], in0=OT[:, qb, :], scalar=RS4[:, 4 + qb:5 + qb],
                                                                                          in1=CST[:, C_GA + h * 128:C_GA + (h + 1) * 128], op0=ALU.mult, op1=ALU.mult),
                                                   reads=[R_ot[qb], R_rs4, R_cst], writes=[R_ya4])) for qb in range(4)])
                rounds.append([])
                rounds.append([(lambda qb=qb: S.op("pe", lambda e: e.transpose(PSB7[:, 512 + qb * 128:512 + (qb + 1) * 128], YA4[:, qb, :], identb[:]),
                                                   reads=[R_ya4, R_identb], writes=[R_b7[1]])) for qb in range(4)])
                rounds.append([])
                rounds.append([lambda: S.op("act", lambda e: e.activation(out=Y[:, h, qc * CH:(qc + 1) * CH], in_=PSB7[:, 512:1024], func=AF.Copy),
                                            reads=[R_b7[1]], writes=[Yres[qc][h]])])
                return rounds

            for u in prep_units(0, 0):
                u()
            rounds = []
            for h in range(4):
                st_ = h % 2
                QT, KT, VA, GS = QTs[st_], KTs[st_], VAs[st_], GSs[st_]
                bg = prep_units(h + 1, 1 - st_) if h + 1 < 4 else []
                iters = [(qc, kb) for qc in range(NCH) for kb in range(16)]
                NIT = len(iters)
                LOOK = 3

                def qk(it):
                    qc, kb = iters[it]
                    sp = it % 2
                    for c in range(2):
                        mm(PSALL[:, 2 * sp + c, :], KT[c * 64:(c + 1) * 64, kb * 128:(kb + 1) * 128],
                           QT[c * 64:(c + 1) * 64, qc * CH:(qc + 1) * CH], True, True,
                           [R_kts[st_][kb // 4], R_qts[st_][qc]], [PSR[2 * sp + c]])
                    x2 = it % 2
                    x3 = it % 4
                    S.op("act", lambda e: e.activation(out=EX[x2], in_=PSALL[:, 2 * sp:2 * sp + 2, :], func=AF.Exp, scale=0.125),
                         reads=[PSR[2 * sp], PSR[2 * sp + 1]], writes=[R_ex[x2]])
                    off = 1920 + qc * CH - kb * 128
                    gres = [R_gss[st_][i] for i in range(off // 512, min(7, (off + 511) // 512) + 1)]
                    gwin = GS[:, off:off + 512].unsqueeze(1).to_broadcast([128, 2, 512])
                    S.op("dve", lambda e: e.tensor_tensor(out=EG[x3], in0=EX[x2], in1=gwin, op=ALU.mult),
                         reads=[R_ex[x2]] + gres, writes=[R_eg[x3]])

                def av(it):
                    qc, kb = iters[it]
                    x3 = it % 4
                    for c in range(2):
                        for qb in range(4):
                            a = c * 4 + qb
                            mm(acc(a), EG[x3][:, c, qb * 128:(qb + 1) * 128], VA[:, kb, :], kb == 0 and a % 3 == 0, kb == 15,
                               [R_eg[x3], R_vas[st_][kb // 2]], [PSR[4 + a // 3]], sgc=True)
                    if kb == 15:
                        S.op("dve", lambda e: e.tensor_copy(out=ACCS[:, 0:3, :], in_=PSALL[:, 4, 0:387].rearrange("p (a b) -> p a b", b=129)),
                             reads=[PSR[4]], writes=[R_accs])
                        S.op("dve", lambda e: e.tensor_copy(out=ACCS[:, 3:6, :], in_=PSALL[:, 5, 0:387].rearrange("p (a b) -> p a b", b=129)),
                             reads=[PSR[5]], writes=[R_accs])
                        S.op("dve", lambda e: e.tensor_copy(out=ACCS[:, 6:8, :], in_=PSALL[:, 6, 0:258].rearrange("p (a b) -> p a b", b=129)),
                             reads=[PSR[6]], writes=[R_accs])
                        rounds.extend(epilogue_rounds(qc, h))

                for t in range(NIT + LOOK):
                    if t < NIT:
                        qk(t)
                    if t >= LOOK:
                        av(t - LOOK)
                    if rounds:
                        for th in rounds.pop(0):
                            th()
                    if bg and t >= 4 and t % 2 == 0 and len(rounds) not in (1, 2, 3):
                        bg.pop(0)()
                while bg:
                    bg.pop(0)()
            while rounds:
                for th in rounds.pop(0):
                    th()

        arena_pos["o"] = 0
        BLK = [carve([2052], F32) for _ in range(3)]
        R_blk = [Res(), Res(), Res()]
        GA_, GB_, GC_ = BLK[0][:, 0:SEQ], BLK[1][:, 0:SEQ], BLK[2][:, 0:SEQ]
        PRE, ACCV = BLK[0][:, 0:2050], BLK[1][:, 0:SEQ]
        HACCs = [BLK[1][:, 0:SEQ].rearrange("p (a b) -> p a b", b=128), BLK[2][:, 0:SEQ].rearrange("p (a b) -> p a b", b=128)]
        TG = carve([16, 72], F32)
        DECB = carve([8, 16], F32)
        QTp = carve([SEQ], BF16)
        KTp = carve([SEQ], BF16)
        KTOK = carve([16, 128], BF16)
        VW = [carve([16, 129], BF16) for _ in range(2)]
        PT2 = [carve([2, 128], BF16) for _ in range(2)]
        CB3 = [[carve([129], BF16) for _ in range(3)] for _ in range(2)]
        MST = [carve([129], F32) for _ in range(2)]
        DEN = carve([8], F32)
        OGs = [carve([16, 128], BF16) for _ in range(2)]
        T2 = carve([128], F32)
        YM4 = carve([4, 128], BF16)
        SQM = carve([128], F32)
        RBT = carve([48], F32)
        SSHs = [carve([48], F32) for _ in range(2)]
        NBG = sb("nbg", [128, 2], F32)
        ONES1 = sb("ones1", [128, 1], F32)
        R_tg, R_decb, R_qtp, R_ktp = Res(), Res(), Res(), Res()
        R_ktok = [Res() for _ in range(16)]
        R_vw = [[Res() for _ in range(4)] for _ in range(2)]
        R_pt2, R_mst, R_den = [Res(), Res()], [Res(), Res()], [Res(), Res()]
        R_cb3 = [[Res() for _ in range(3)] for _ in range(2)]
        R_haccs = [[Res() for _ in range(16)] for _ in range(2)]
        R_ogs = [[Res() for _ in range(4)] for _ in range(2)]
        R_sshs = [Res(), Res()]
        R_og, R_t2, R_ym, R_sqm, R_rbt, R_ssh, R_nbg, R_ones1 = [Res() for _ in range(8)]
        PSB6 = PSALL[:, 6, :].bitcast(BF16)
        S.op("dve", lambda e: e.tensor_scalar(out=NBG[:, 0:1], in0=CST[:, C_BF:C_BF + 1], scalar1=-1.0, scalar2=None, op0=ALU.mult),
             reads=[R_cst], writes=[R_nbg])
        S.op("dve", lambda e: e.memset(ONES1[:], 1.0), writes=[R_ones1])

        def rev(ap):
            a = ap.ap
            assert len(a) == 2, a
            return bass.AP(ap.tensor, ap.offset + (a[1][1] - 1) * a[1][0], [[a[0][0], a[0][1]], [-a[1][0], a[1][1]]])

        def strided(ap, start, step, n):
            a = ap.ap
            assert len(a) == 2, a
            return bass.AP(ap.tensor, ap.offset + start * a[1][0], [[a[0][0], a[0][1]], [a[1][0] * step, n]])

        def mlstm(s, dbg=False):
            ident = CST[:, C_ID:C_ID + 128]
            pending = []
            gsl, gr = loadA(s_win[WIN_IDX["gate"]], R_win["gate"])
            for j in range(NCH):
                for dc in range(8):
                    mm(PSALL[0:36, 0, :], gsl[:, dc, 0:36], uap(j, dc), dc == 0, dc == 7, [gr, Ures[j][dc]], [PSR[0]])
                for dc in range(8):
                    mm(PSALL[0:36, 1, :], gsl[:, dc, 64:100], uap(j, dc), dc == 0, dc == 7, [gr, Ures[j][dc]], [PSR[1]])
                S.op("act", lambda e, j=j: e.activation(out=GA_[0:36, j * CH:(j + 1) * CH], in_=PSALL[0:36, 0, :], func=AF.Identity,
                                                        bias=CST[0:36, C_BI:C_BI + 1]),
                     reads=[PSR[0], R_cst], writes=[R_blk[0]])
                S.op("act", lambda e, j=j: e.activation(out=GB_[0:36, j * CH:(j + 1) * CH], in_=PSALL[0:36, 1, :], func=AF.Exp,
                                                        scale=-1.0, bias=NBG[0:36, 0:1]),
                     reads=[PSR[1], R_nbg], writes=[R_blk[1]])
            S.op("act", lambda e: e.activation(out=GB_[0:36, :], in_=GB_[0:36, :], func=AF.Ln, bias=1.0), reads=[R_blk[1]], writes=[R_blk[1]])
            if dbg:
                allr = [r_ for l_ in Hres for r_ in l_]
                S.op("dve", lambda e: e.tensor_copy(out=H[0:36, 6, :], in_=GA_[0:36, :]), reads=[R_blk[0]] + allr, writes=allr)
                S.op("dve", lambda e: e.tensor_copy(out=H[0:36, 7, :], in_=GB_[0:36, :]), reads=[R_blk[1]] + allr, writes=allr)
            S.op("dve", lambda e: e.memset(GC_[0:36, :], 0.0), writes=[R_blk[2]])
            S.op("dve", lambda e: e.tensor_tensor_scan(out=GC_[0:4, :], data0=ONES1[0:4, 0:1].to_broadcast([4, SEQ]), data1=GB_[0:4, :],
                                                       initial=0.0, op0=ALU.mult, op1=ALU.add),
                 reads=[R_blk[1], R_ones1], writes=[R_blk[2]])
            S.op("dve", lambda e: e.tensor_tensor_scan(out=rev(GC_[32:36, :]), data0=ONES1[32:36, 0:1].to_broadcast([4, SEQ]), data1=rev(GB_[32:36, :]),
                                                       initial=0.0, op0=ALU.mult, op1=ALU.add),
                 reads=[R_blk[1], R_ones1], writes=[R_blk[2]])
            S.op("dve", lambda e: e.tensor_tensor(out=GA_[0:36, :], in0=GA_[0:36, :], in1=GC_[0:36, :], op=ALU.add),
                 reads=[R_blk[0], R_blk[2]], writes=[R_blk[0]])
            S.op("dve", lambda e: e.tensor_tensor_scan(out=GB_[0:4, :], data0=ONES1[0:4, 0:1].to_broadcast([4, SEQ]), data1=GA_[0:4, :],
                                                       initial=0.0, op0=ALU.mult, op1=ALU.max),
                 reads=[R_blk[0], R_ones1], writes=[R_blk[1]])
            S.op("dve", lambda e: e.tensor_tensor_scan(out=rev(GB_[32:36, :]), data0=ONES1[32:36, 0:1].to_broadcast([4, SEQ]), data1=rev(GA_[32:36, :]),
                                                       initial=0.0, op0=ALU.mult, op1=ALU.max),
                 reads=[R_blk[0], R_ones1], writes=[R_blk[1]])
            S.op("dve", lambda e: e.memset(RBT[0:36, 0:32], 0.0), writes=[R_rbt])
            S.op("dve", lambda e: e.tensor_copy(out=RBT[0:4, 1:16], in_=strided(GB_[0:4, :], 127, 128, 15)), reads=[R_blk[1]], writes=[R_rbt])
            S.op("dve", lambda e: e.tensor_copy(out=RBT[32:36, 0:15], in_=strided(GB_[32:36, :], 128, 128, 15)), reads=[R_blk[1]], writes=[R_rbt])
            S.op("dve", lambda e: e.tensor_copy(out=RBT[0:4, 16:32], in_=strided(GB_[0:4, :], 127, 128, 16)), reads=[R_blk[1]], writes=[R_rbt])
            S.op("dve", lambda e: e.tensor_copy(out=RBT[32:36, 16:32], in_=strided(GB_[32:36, :], 0, 128, 16)), reads=[R_blk[1]], writes=[R_rbt])
            S.op("dve", lambda e: e.tensor_tensor(out=RBT[0:36, 32:48], in0=RBT[0:36, 0:16], in1=RBT[0:36, 16:32], op=ALU.subtract),
                 reads=[R_rbt], writes=[R_rbt])
            S.op("act", lambda e: e.activation(out=RBT[0:36, 32:48], in_=RBT[0:36, 32:48], func=AF.Exp), reads=[R_rbt], writes=[R_rbt])
            for (buf, rb) in ((GA_, R_blk[0]), (GC_, R_blk[2])):
                v3 = buf[0:36, :].rearrange("p (a b) -> p a b", b=128)
                S.op("dve", lambda e, v3=v3: e.tensor_tensor(out=v3, in0=v3, in1=RBT[0:36, 0:16].unsqueeze(2).to_broadcast([36, 16, 128]), op=ALU.subtract),
                     reads=[rb, R_rbt], writes=[rb])
                S.op("act", lambda e, buf=buf: e.activation(out=buf[0:36, :], in_=buf[0:36, :], func=AF.Exp), reads=[rb], writes=[rb])
            for t4 in range(4):
                for tq in range(4):
                    tt = t4 * 4 + tq
                    S.op("pe", lambda e, tt=tt, tq=tq: e.transpose(PSALL[:, 2, tq * 72:tq * 72 + 36], GA_[0:36, tt * 128:(tt + 1) * 128], ident[0:36, 0:36]),
                         reads=[R_blk[0], R_cst], writes=[PSR[2]])
                    S.op("pe", lambda e, tt=tt, tq=tq: e.transpose(PSALL[:, 2, tq * 72 + 36:tq * 72 + 72], GC_[0:36, tt * 128:(tt + 1) * 128], ident[0:36, 0:36]),
                         reads=[R_blk[2], R_cst], writes=[PSR[2]])
                S.op("act", lambda e, t4=t4: e.activation(out=TG[:, t4 * 4:(t4 + 1) * 4, :], in_=PSALL[:, 2, 0:288].rearrange("p (a b) -> p a b", b=72), func=AF.Copy),
                     reads=[PSR[2]], writes=[R_tg])
            for dr in range(2):
                for h in range(4):
                    ch = dr * 4 + h
                    mm(PSALL[:, 3, ch * 16:(ch + 1) * 16], CST[dr * 32:dr * 32 + 4, C_SEL + h * 128:C_SEL + (h + 1) * 128],
                       RBT[dr * 32:dr * 32 + 4, 32:48], True, True, [R_cst, R_rbt], [PSR[3]])
            S.op("dve", lambda e: e.tensor_copy(out=DECB, in_=PSALL[:, 3, 0:128].rearrange("p (a b) -> p a b", b=16)), reads=[PSR[3]], writes=[R_decb])

            for g in range(2):
                for (nm, dstT, dres, ccol, isk) in (("mq%d" % g, QTp, R_qtp, C_CQ + 3 * g, False), ("mk%d" % g, KTp, R_ktp, C_CK + 3 * g, True)):
                    wt, wr = loadA(s_win[WIN_IDX[nm]], R_win[nm])
                    S.op("dve", lambda e: e.memset(PRE[:, 0:1], 0.0), writes=[R_blk[0]])
                    S.op("dve", lambda e: e.memset(PRE[:, 2049:2050], 0.0), writes=[R_blk[0]])
                    for j in range(NCH):
                        b = 6 + j % 2
                        for dc in range(8):
                            mm(PS[b], wt[:, dc, :], uap(j, dc), dc == 0, dc == 7, [wr, Ures[j][dc]], [PSR[b]])
                        S.op("act", lambda e, j=j, b=b: e.activation(out=PRE[:, 1 + j * CH:1 + (j + 1) * CH], in_=PS[b], func=AF.Copy),
                             reads=[PSR[b]], writes=[R_blk[0]])
                    S.op("dve", lambda e, ccol=ccol: e.tensor_scalar(out=ACCV, in0=PRE[:, 1:2049], scalar1=CST[:, ccol + 1:ccol + 2], scalar2=None, op0=ALU.mult),
                         reads=[R_blk[0], R_cst], writes=[R_blk[1]])
                    S.op("dve", lambda e, ccol=ccol: e.scalar_tensor_tensor(out=ACCV, in0=PRE[:, 0:2048], scalar=CST[:, ccol:ccol + 1], in1=ACCV, op0=ALU.mult, op1=ALU.add),
                         reads=[R_blk[0], R_blk[1], R_cst], writes=[R_blk[1]])
                    S.op("dve", lambda e, ccol=ccol: e.scalar_tensor_tensor(out=ACCV, in0=PRE[:, 2:2050], scalar=CST[:, ccol + 2:ccol + 3], in1=ACCV, op0=ALU.mult, op1=ALU.add),
                         reads=[R_blk[0], R_blk[1], R_cst], writes=[R_blk[1]])
                    if not isk:
                        S.op("act", lambda e, dstT=dstT: e.activation(out=dstT, in_=ACCV, func=AF.Silu), reads=[R_blk[1]], writes=[dres])
                    else:
                        S.op("act", lambda e: e.activation(out=ACCV, in_=ACCV, func=AF.Silu), reads=[R_blk[1]], writes=[R_blk[1]])
                        S.op("dve", lambda e, dstT=dstT: e.tensor_scalar(out=dstT, in0=ACCV, scalar1=0.125, scalar2=None, op0=ALU.mult),
                             reads=[R_blk[1]], writes=[dres])
                for tt in range(16):
                    S.op("pe", lambda e, tt=tt: e.transpose(PSB6[:, (tt % 4) * 128:(tt % 4 + 1) * 128], KTp[:, tt * 128:(tt + 1) * 128], identb[:]),
                         reads=[R_ktp, R_identb], writes=[PSR[6]])
                    if tt % 4 == 3:
                        t4 = tt // 4
                        S.op("dve", lambda e, t4=t4: e.tensor_copy(out=KTOK[:, t4 * 4:(t4 + 1) * 4, :], in_=PSB6[:, 0:512].rearrange("p (a b) -> p a b", b=128)),
                             reads=[PSR[6]], writes=R_ktok[t4 * 4:(t4 + 1) * 4])
                for hh in range(2):
                    h = 2 * g + hh
                    P0 = hh * 64
                    vt, vr = loadA(s_win[WIN_IDX["mv%d" % h]], R_win["mv%d" % h])
                    for t4 in range(4):
                        for tq in range(4):
                            tt = t4 * 4 + tq
                            for dc in range(8):
                                mm(PSALL[:, 7, tq * 128:(tq + 1) * 128], U[:, dc, tt * 128:(tt + 1) * 128], vt[:, dc, :], dc == 0, dc == 7,
                                   [vr, Ures[t4][dc]], [PSR[7]])
                        for dr in range(2):
                            col = dr * 32 + h
                            S.op("dve", lambda e, t4=t4, dr=dr, col=col: e.tensor_tensor(
                                out=VW[dr][:, t4 * 4:(t4 + 1) * 4, 0:128], in0=PS[7].rearrange("p (a b) -> p a b", b=128),
                                in1=TG[:, t4 * 4:(t4 + 1) * 4, col:col + 1].to_broadcast([128, 4, 128]), op=ALU.mult),
                                reads=[PSR[7], R_tg], writes=[R_vw[dr][t4]])
                    for dr in range(2):
                        col = dr * 32 + h
                        S.op("act", lambda e, dr=dr, col=col: e.activation(out=VW[dr][:, :, 128:129], in_=TG[:, :, col:col + 1], func=AF.Copy),
                             reads=[R_tg], writes=R_vw[dr])
                    hb = h % 2
                    while pending and pending[0][0] <= h - 2:
                        for th in pending.pop(0)[1]:
                            th()
                    HACC = HACCs[hb]
                    R_hacc = R_haccs[hb]
                    rblk = R_blk[1 + hb]
                    S.op("dve", lambda e, hb=hb: e.memset(BLK[1 + hb][:, 0:SEQ], 0.0), writes=[rblk] + R_hacc)
                    def tt_of(i, dr):
                        return i if dr == 0 else 15 - i

                    def sk_pe(i, h=h, P0=P0):
                        par = i % 2
                        for dr in range(2):
                            tt = tt_of(i, dr)
                            tc_ = slice(tt * 128, (tt + 1) * 128)
                            mm(PSALL[:, par, dr * 128:(dr + 1) * 128], KTp[P0:P0 + 64, tc_], QTp[P0:P0 + 64, tc_], True, True, [R_ktp, R_qtp], [PSR[par]])
                        for dr in range(2):
                            tt = tt_of(i, dr)
                            mm(PSALL[:, 4 + par, dr * 129:(dr + 1) * 129], KTOK[:, tt, :], VW[dr][:, tt, :], True, True,
                               [R_ktok[tt], R_vw[dr][tt // 4]], [PSR[4 + par]])

                    def sk_dve(i, h=h, P0=P0):
                        par = i % 2
                        S.op("dve", lambda e, par=par: e.tensor_tensor(out=PT2[par], in0=PSALL[:, par, 0:256].rearrange("p (a b) -> p a b", b=128),
                                                                     in1=CST[:, C_MU:C_MU + 256].rearrange("p (a b) -> p a b", b=128), op=ALU.mult),
                             reads=[PSR[par], R_cst], writes=[R_pt2[par]])
                        for dr in range(2):
                            ch = dr * 4 + h
                            kv = PSALL[P0:P0 + 64, 4 + par, dr * 129:(dr + 1) * 129]
                            if i == 0:
                                S.op("dve", lambda e, dr=dr, kv=kv: e.tensor_copy(out=MST[dr][P0:P0 + 64, :], in_=kv),
                                     reads=[PSR[4 + par]], writes=[R_mst[dr]])
                            else:
                                pt_ = tt_of(i - 1, dr)
                                S.op("dve", lambda e, dr=dr, kv=kv, ch=ch, pt_=pt_: e.scalar_tensor_tensor(
                                    out=MST[dr][P0:P0 + 64, :], in0=MST[dr][P0:P0 + 64, :], scalar=DECB[P0:P0 + 64, ch, pt_:pt_ + 1],
                                    in1=kv, op0=ALU.mult, op1=ALU.add),
                                    reads=[PSR[4 + par], R_mst[dr], R_decb], writes=[R_mst[dr]])

                    def sk_cb(i, h=h, P0=P0):
                        if i >= 15:
                            return
                        g3 = i % 3
                        for dr in range(2):
                            tt = tt_of(i, dr)
                            ch = dr * 4 + h
                            S.op("act", lambda e, dr=dr, ch=ch, tt=tt, g3=g3: e.activation(out=CB3[dr][g3][P0:P0 + 64, :], in_=MST[dr][P0:P0 + 64, :], func=AF.Copy,
                                                                                     scale=DECB[P0:P0 + 64, ch, tt:tt + 1]),
                                 reads=[R_mst[dr], R_decb], writes=[R_cb3[dr][g3]])

                    def a_pe(i, h=h, P0=P0):
                        par = i % 2
                        first = i == 0
                        for dr in range(2):
                            tt = tt_of(i, dr)
                            tc_ = slice(tt * 128, (tt + 1) * 128)
                            oa = PSALL[:, 2 + par, dr * 129:(dr + 1) * 129]
                            mm(oa, PT2[par][:, dr, :], VW[dr][:, tt, :], True, first, [R_pt2[par], R_vw[dr][tt // 4]], [PSR[2 + par]])
                            if not first:
                                g3 = (i - 1) % 3
                                mm(oa, QTp[P0:P0 + 64, tc_], CB3[dr][g3][P0:P0 + 64, :], False, True, [R_qtp, R_cb3[dr][g3]], [PSR[2 + par]])
                        S.op("act", lambda e, par=par: e.activation(out=DEN[:, 0:2].unsqueeze(2),
                                                                    in_=PSALL[:, 2 + par, 0:258].rearrange("p (a b) -> p a b", b=129)[:, :, 128:129], func=AF.Abs),
                             reads=[PSR[2 + par]], writes=[R_den[0]])

                    def a_out(i, h=h, P0=P0, HACC=HACC, R_hacc=R_hacc, rblk=rblk):
                        par = i % 2
                        for dr in range(2):
                            tt = tt_of(i, dr)
                            clc = 36 + dr * 32 + h
                            S.op("dve", lambda e, dr=dr, tt=tt, clc=clc: e.tensor_tensor(out=DEN[:, 2 + dr:3 + dr], in0=DEN[:, dr:dr + 1], in1=TG[:, tt, clc:clc + 1], op=ALU.max),
                                 reads=[R_den[0], R_tg], writes=[R_den[0]])
                        S.op("dve", lambda e: e.reciprocal(out=DEN[:, 4:6], in_=DEN[:, 2:4]), reads=[R_den[0]], writes=[R_den[0]])
                        for dr in range(2):
                            tt = tt_of(i, dr)
                            S.op("dve", lambda e, dr=dr, tt=tt, par=par: e.scalar_tensor_tensor(
                                out=HACC[:, tt, :], in0=PSALL[:, 2 + par, dr * 129:dr * 129 + 128], scalar=DEN[:, 4 + dr:5 + dr], in1=HACC[:, tt, :], op0=ALU.mult, op1=ALU.add),
                                reads=[PSR[2 + par], R_den[0], R_hacc[tt], rblk], writes=[R_hacc[tt]])


                    OG, R_og4, SSH, R_ssh = OGs[hb], R_ogs[hb], SSHs[hb], R_sshs[hb]
                    oslab = {}

                    def oproj(t4, h=h, OG=OG, R_og4=R_og4):
                        if "s" not in oslab:
                            oslab["s"] = loadA(s_win[WIN_IDX["mo%d" % h]], R_win["mo%d" % h])
                        ot, orr = oslab["s"]
                        for tq in range(4):
                            tt = t4 * 4 + tq
                            for dc in range(8):
                                mm(PSALL[:, 6, tq * 128:(tq + 1) * 128], U[:, dc, tt * 128:(tt + 1) * 128], ot[:, dc, :], dc == 0, dc == 7,
                                   [orr, Ures[t4][dc]], [PSR[6]])
                        S.op("act", lambda e: e.activation(out=OG[:, t4 * 4:(t4 + 1) * 4, :], in_=PS[6].rearrange("p (a b) -> p a b", b=128), func=AF.Sigmoid),
                             reads=[PSR[6]], writes=[R_og4[t4]])
                    bg = [(lambda t4=t4: oproj(t4)) for t4 in range(4)]

                    def ssq(tt, HACC=HACC, R_hacc=R_hacc, rblk=rblk, SSH=SSH, R_ssh=R_ssh):
                        S.op("act", lambda e: e.activation(out=SQM, in_=HACC[:, tt, :], func=AF.Square, accum_out=SSH[:, tt:tt + 1]),
                             reads=[R_hacc[tt], rblk], writes=[R_sqm, R_ssh])

                    def pop_pending(only_dve):
                        if not pending:
                            return False
                        if only_dve and pending[0][2] != "dve":
                            return False
                        for th in pending.pop(0)[1]:
                            th()
                        return True

                    sk_pe(0)
                    sk_dve(0)
                    sk_cb(0)
                    for i in range(16):
                        if i + 1 < 16:
                            sk_pe(i + 1)
                        a_pe(i)
                        if i + 1 < 16:
                            sk_dve(i + 1)
                            sk_cb(i + 1)
                        pop_pending(True)
                        a_out(i)
                        if i >= 8:
                            ssq(i)
                            ssq(15 - i)
                        if bg:
                            bg.pop(0)()
                        else:
                            pop_pending(False)
                    while bg:
                        bg.pop(0)()
                    S.op("act", lambda e, SSH=SSH: e.activation(out=SSH[:, 16:32], in_=SSH[:, 0:16], func=AF.Sqrt, scale=1.0 / 128, bias=EPS), reads=[R_ssh], writes=[R_ssh])
                    S.op("dve", lambda e, SSH=SSH: e.reciprocal(out=SSH[:, 32:48], in_=SSH[:, 16:32]), reads=[R_ssh], writes=[R_ssh])

                    def final_rounds(h=h, HACC=HACC, R_hacc=R_hacc, rblk=rblk, OG=OG, R_og4=R_og4, SSH=SSH, R_ssh=R_ssh):
                        rounds = []
                        for t4 in range(4):
                            for tq in range(4):
                                tt = t4 * 4 + tq
                                rounds.append(("dve", [
                                    lambda tt=tt: S.op("dve", lambda e: e.scalar_tensor_tensor(out=T2, in0=HACC[:, tt, :], scalar=SSH[:, 32 + tt:33 + tt],
                                                                                              in1=CST[:, C_GM + h * 128:C_GM + (h + 1) * 128], op0=ALU.mult, op1=ALU.mult),
                                                       reads=[R_hacc[tt], rblk, R_ssh, R_cst], writes=[R_t2]),
                                    lambda tt=tt, tq=tq, t4=t4: S.op("dve", lambda e: e.tensor_tensor(out=YM4[:, tq, :], in0=T2, in1=OG[:, tt, :], op=ALU.mult),
                                                                      reads=[R_t2, R_og4[t4]], writes=[R_ym])]))
                            rounds.append(("pe", [(lambda tq=tq: S.op("pe", lambda e: e.transpose(PSB7[:, tq * 128:(tq + 1) * 128], YM4[:, tq, :], identb[:]),
                                                                      reads=[R_ym, R_identb], writes=[PSR[7]])) for tq in range(4)]))
                            rounds.append(("act", [lambda t4=t4: S.op("act", lambda e: e.activation(out=Y[:, h, t4 * CH:(t4 + 1) * CH], in_=PSB7[:, 0:512], func=AF.Copy),
                                                                      reads=[PSR[7]], writes=[Yres[t4][h]])]))
                        return rounds
                    pending.extend([(h, r_[1], r_[0]) for r_ in final_rounds()])
            while pending:
                for th in pending.pop(0)[1]:
                    th()


        finals = []
        nseq = 1 if stage in ("att", "mlstm", "attdbg", "mldbg") else 2
        for s in range(nseq):
            for j in range(NCH):
                load_x(s, j)
                last = phase_a(s, j)
                if s == 0 and j == 0:
                    tok = Res()
                    tok.last_w = last
                    gate_tok["r"] = [tok]
                    conv_stage2()
                    gate_tok["r"] = []
                if s == 0 and j == 2:
                    gen_strips()
                    tok = Res()
                    tok.last_w = last
                    gate_tok["r"] = [tok]
                    conv_stage3()
                    gate_tok["r"] = []
                if stage == "A":
                    finals.append(store(s, j))
            if stage == "A":
                continue
            if stage in ("mlstm", "mldbg"):
                S.barrier(arena_dma_reads)
                mlstm(s, dbg=(stage == "mldbg"))
                for j in range(NCH):
                    if stage == "mlstm":
                        for hh in range(4):
                            S.op("dve", lambda e, j=j, hh=hh: e.tensor_copy(out=hap(j, hh), in_=Y[:, hh, j * CH:(j + 1) * CH]),
                                 reads=[Yres[j][hh], Hres[j][hh]], writes=[Hres[j][hh]])
                    finals.append(store(s, j))
                continue
            if stage != "AC":
                S.barrier(arena_dma_reads)
                attention(s)
                if stage == "attdbg":
                    for j in range(NCH):
                        finals.append(store(s, j))
                    continue
                if stage == "att":
                    for j in range(NCH):
                        c0 = j * CH
                        for hh in range(4):
                            S.op("dve", lambda e, j=j, hh=hh: e.tensor_copy(out=hap(j, hh), in_=Y[:, hh, j * CH:(j + 1) * CH]),
                                 reads=[Yres[j][hh], Hres[j][hh]], writes=[Hres[j][hh]])
                        finals.append(store(s, j))
                    continue
                S.barrier(arena_dma_reads)
                for j in range(NCH):
                    wout_proj(j, 1)
                S.barrier(arena_dma_reads)
                mlstm(s)
                S.barrier(arena_dma_reads)
                for j in range(NCH):
                    wout_proj(j, 0)
                S.barrier(arena_dma_reads)
            for j in range(NCH):
                finals.append(phase_c(s, j))
        print("sched stats", S.stats(), "dsems", S.ndsem)
        S.emit_all(final_ops=finals)
    return nc


def kernel(**inp):
    return run_kernel(inp, "full")


def run_kernel(inp, stage="full", ncores=NCORES):
    x = np.asarray(inp["x"], np.float32)
    p = np.asarray(inp["p"], np.float32)[0]
    cst = pack_consts({k: np.asarray(v, np.float32) for k, v in inp.items() if k not in ("x", "p")})
    zer = np.zeros((128, 1024), np.float32)
    shared = {
        "w1i": np.ascontiguousarray(inp["w_ffn1_in"][0], np.float32), "w1o": np.ascontiguousarray(inp["w_ffn1_out"][0], np.float32),
        "w2i": np.ascontiguousarray(inp["w_ffn2_in"][0], np.float32), "w2o": np.ascontiguousarray(inp["w_ffn2_out"][0], np.float32),
        "win": np.ascontiguousarray(inp["w_in"][0], np.float32), "wout": np.ascontiguousarray(inp["w_out"][0], np.float32),
        "wpg": np.ascontiguousarray(inp["w_ple_gate"][0], np.float32), "wpp": np.ascontiguousarray(inp["w_ple_proj"][0], np.float32),
        "cst": cst, "zer": zer,
    }
    in_maps = []
    for c in range(ncores):
        xs = x[2 * c:2 * c + 2].reshape(TOK, D)
        ps = p[2 * c:2 * c + 2].reshape(TOK, 256)
        m = dict(shared)
        m["xT"] = np.ascontiguousarray(xs.T)
        m["pT"] = np.ascontiguousarray(ps.T)
        in_maps.append(m)
    nc = build_nc(stage)
    res = run_bass_kernel_spmd(nc, in_maps, core_ids=list(range(ncores)))
    out = np.empty((2 * ncores, SEQ, D), np.float32)
    for c in range(ncores):
        out[2 * c:2 * c + 2] = np.ascontiguousarray(res.results[c]["outT"].T).reshape(2, SEQ, D)
    return out
```

```python
import contextlib
import numpy as np
import concourse.bass as bass
import concourse.mybir as mybir
from concourse.bass_utils import run_bass_kernel_spmd

F32 = mybir.dt.float32
BF16 = mybir.dt.bfloat16
AF = mybir.ActivationFunctionType
ALU = mybir.AluOpType

NCORES = 8
D = 1024
DFF = 2816
NF = DFF // 128
SEQ = 2048
TOK = 2 * SEQ
CH = 512
NCH = SEQ // CH
EPS = 1e-6
LAM_INIT = 0.8 - 0.6


class Res:
    __slots__ = ("name", "last_w", "readers", "dsem")

    def __init__(self, name=""):
        self.name = name
        self.last_w = None
        self.readers = []
        self.dsem = None


class Op:
    __slots__ = ("eng", "idx", "emit", "waits", "know", "dma", "dsem", "dval", "flag")


class Sched:
    ENG = ("pe", "act", "dve", "pool", "sp")

    def __init__(self, nc):
        self.nc = nc
        self.ops = {e: [] for e in self.ENG}
        self.cidx = {e: 0 for e in self.ENG}
        self.know = {e: {} for e in self.ENG}
        self.ndsem = 0
        self.dcnt = {}

    def new_dsem(self):
        i = self.ndsem
        self.ndsem += 1
        self.dcnt[i] = 0
        return i

    def op(self, eng, emit, reads=(), writes=(), dma=False, dsem=None):
        o = Op()
        o.eng = eng
        o.emit = emit
        o.dma = dma
        o.flag = False
        if dma:
            if dsem is None:
                r0 = writes[0]
                if r0.dsem is None:
                    r0.dsem = self.new_dsem()
                dsem = r0.dsem
            o.dsem = dsem
            self.dcnt[dsem] += 16
            o.dval = self.dcnt[dsem]
            o.idx = -1
        else:
            o.idx = self.cidx[eng]
            self.cidx[eng] += 1
        deps = []
        for r in reads:
            if r.last_w is not None:
                deps.append(r.last_w)
        for r in writes:
            if r.last_w is not None:
                deps.append(r.last_w)
            deps.extend(r.readers)
        know = self.know[eng]
        waits = {}
        eff = []
        for d in deps:
            if d is o:
                continue
            if d.dma:
                if dma and d.dsem == o.dsem:
                    continue
                key, val = ("d", d.dsem), d.dval
            else:
                if d.eng == eng and not dma:
                    if eng == "pe":
                        continue
                key, val = ("e", d.eng), d.idx + 1
            eff.append(d)
            if know.get(key, 0) >= val:
                continue
            if waits.get(key, (0, None))[0] < val:
                waits[key] = (val, d)
        for d in eff:
            for k, v in d.know.items():
                if know.get(k, 0) < v:
                    know[k] = v
        for key, (val, d) in waits.items():
            if know.get(key, 0) < val:
                know[key] = val
            d.flag = True
        o.waits = [(key, d) for key, (val, d) in sorted(waits.items(), key=lambda kv: str(kv[0]))]
        o.know = dict(know)
        self.ops[eng].append(o)
        for r in reads:
            r.readers.append(o)
        for r in writes:
            r.last_w = o
            r.readers = []
        return o

    def barrier(self, extra_ops=()):
        toks = []
        for o in extra_ops:
            r = Res()
            r.last_w = o
            toks.append(r)
        for e in ("pe", "act", "dve"):
            last = None
            for o in reversed(self.ops[e]):
                if not o.dma:
                    last = o
                    break
            if last is not None:
                r = Res()
                r.last_w = last
                toks.append(r)
        for e in self.ENG:
            self.op(e, lambda eh: eh.nop(), reads=toks)

    def stats(self):
        return {e: (len(self.ops[e]), sum(len(o.waits) for o in self.ops[e])) for e in self.ENG}

    def emit_all(self, final_ops=()):
        nc = self.nc
        cnt = {}
        for e in self.ENG:
            c = 0
            for o in self.ops[e]:
                if o.dma:
                    continue
                if o.flag:
                    c += 1
                cnt[(e, o.idx)] = c
        with contextlib.ExitStack() as st:
            esem = {e: st.enter_context(nc.semaphore("s_" + e)) for e in self.ENG}
            dsem = [st.enter_context(nc.semaphore("d%d" % i)) for i in range(self.ndsem)]
            block = st.enter_context(nc.Block())

            def emit_wait(eh, key, d):
                if key[0] == "d":
                    eh.wait_ge(dsem[key[1]], d.dval)
                else:
                    eh.wait_ge(esem[key[1]], cnt[(d.eng, d.idx)])

            def run(engname, eh):
                for o in self.ops[engname]:
                    for key, d in o.waits:
                        emit_wait(eh, key, d)
                    ins = o.emit(eh)
                    if o.dma:
                        ins.then_inc(dsem[o.dsem], 16)
                    elif o.flag:
                        ins.then_inc(esem[engname], 1)

            @block.tensor
            def _(e):
                run("pe", e)

            @block.scalar
            def _(e):
                run("act", e)

            @block.vector
            def _(e):
                run("dve", e)

            @block.gpsimd
            def _(e):
                run("pool", e)
                done = {}
                for o in final_ops:
                    done[o.dsem] = max(done.get(o.dsem, 0), o.dval)
                for k, v in done.items():
                    e.wait_ge(dsem[k], v)

            @block.sync
            def _(e):
                run("sp", e)


C_G = 0
C_CQ = C_G + 40
C_CK = C_CQ + 6
C_BI = C_CK + 6
C_BF = C_BI + 1
C_LAM = C_BF + 1
C_GM = C_LAM + 256
C_GA = C_GM + 512
C_ID = C_GA + 512
C_MU = C_ID + 128
C_ML = C_MU + 128
C_SEL = C_ML + 128
NCST = C_SEL + 512


def pack_consts(inp):
    c = np.zeros((128, NCST), np.float32)
    for gi, k in enumerate(("g_ffn1", "g_mix", "g_ffn2", "g_ple")):
        c[:, C_G + gi * 8:C_G + gi * 8 + 8] = inp[k][0].reshape(8, 128).T
    c[:, C_G + 32:C_G + 40] = inp["g_final"].reshape(8, 128).T
    cw = inp["conv_w"][0]
    for g in range(2):
        for k in range(3):
            c[:, C_CQ + g * 3 + k] = cw[k, g * 128:(g + 1) * 128]
            c[:, C_CK + g * 3 + k] = cw[k, 256 + g * 128:256 + (g + 1) * 128]
    b = inp["b_mgate"][0]
    c[0:4, C_BI] = b[0:4]
    c[32:36, C_BI] = b[8:12]
    c[0:4, C_BF] = b[4:8]
    c[32:36, C_BF] = b[12:16]
    for i, k in enumerate(("lam_q1", "lam_k1", "lam_q2", "lam_k2")):
        c[:, C_LAM + i * 64:C_LAM + (i + 1) * 64] = inp[k][0][None, :]
    c[:, C_GM:C_GM + 512] = inp["g_mnorm"][0][None, :]
    c[:, C_GA:C_GA + 512] = inp["g_anorm"][0][None, :]
    c[:, C_ID:C_ID + 128] = np.eye(128, dtype=np.float32)
    i = np.arange(128)
    c[:, C_MU:C_MU + 128] = (i[:, None] <= i[None, :]).astype(np.float32)
    c[:, C_ML:C_ML + 128] = (i[:, None] >= i[None, :]).astype(np.float32)
    for h in range(4):
        c[h, C_SEL + h * 128:C_SEL + (h + 1) * 128] = 1.0
        c[32 + h, C_SEL + h * 128:C_SEL + (h + 1) * 128] = 1.0
    return c


WIN_SLABS = {}
for _g in range(2):
    WIN_SLABS["mq%d" % _g] = _g * 128
    WIN_SLABS["mk%d" % _g] = 256 + _g * 128
for _h in range(4):
    WIN_SLABS["mv%d" % _h] = 512 + _h * 128
    WIN_SLABS["mo%d" % _h] = 1024 + _h * 128
    WIN_SLABS["aq%d" % _h] = 1552 + _h * 128
    WIN_SLABS["ak%d" % _h] = 2064 + _h * 128
    WIN_SLABS["av%d" % _h] = 2576 + _h * 128
WIN_NAMES = list(WIN_SLABS.keys()) + ["gate"]
WIN_IDX = {n: i for i, n in enumerate(WIN_NAMES)}


def build_nc(stage="full"):
    nc = bass.Bass("TRN2", target_bir_lowering=False)
    S = Sched(nc)

    def dram(name, shape, dt, kind):
        return nc.dram_tensor(name, shape, dt, kind=kind).ap()

    xT = dram("xT", [D, TOK], F32, "ExternalInput")
    pT = dram("pT", [256, TOK], F32, "ExternalInput")
    w1i = dram("w1i", [D, 2 * DFF], F32, "ExternalInput")
    w1o = dram("w1o", [DFF, D], F32, "ExternalInput")
    w2i = dram("w2i", [D, 2 * DFF], F32, "ExternalInput")
    w2o = dram("w2o", [DFF, D], F32, "ExternalInput")
    win = dram("win", [D, 3088], F32, "ExternalInput")
    wout = dram("wout", [D, D], F32, "ExternalInput")
    wpg = dram("wpg", [D, D], F32, "ExternalInput")
    wpp = dram("wpp", [256, D], F32, "ExternalInput")
    cstd = dram("cst", [128, NCST], F32, "ExternalInput")
    zer = dram("zer", [128, 1024], F32, "ExternalInput")
    outT = dram("outT", [D, TOK], F32, "ExternalOutput")
    s_f1i = dram("s_f1i", [2 * NF, 128, 8, 128], BF16, "Internal")
    s_f1o = dram("s_f1o", [8, 128, NF, 128], BF16, "Internal")
    s_f2i = dram("s_f2i", [2 * NF, 128, 8, 128], BF16, "Internal")
    s_f2o = dram("s_f2o", [8, 128, NF, 128], BF16, "Internal")
    s_win = dram("s_win", [len(WIN_NAMES), 128, 8, 128], BF16, "Internal")
    s_wout = dram("s_wout", [2, 8, 128, 4, 128], BF16, "Internal")
    s_wpg = dram("s_wpg", [8, 128, 8, 128], BF16, "Internal")
    s_gs = dram("s_gs", [4, 128, 3968], BF16, "Internal")

    with contextlib.ExitStack() as st:
        def sb(name, shape, dt):
            return st.enter_context(nc.sbuf_tensor(name, shape, dt))

        H = sb("H", [128, 8, SEQ], F32)
        U = sb("U", [128, 8, SEQ], BF16)
        Y = sb("Y", [128, 4, SEQ], BF16)
        NA = 5
        RA = [sb("ra%d" % i, [128, 8, 128], BF16) for i in range(NA)]
        NB_ = 2
        ARENA_F32 = 16384
        ARENA = sb("arena", [128, ARENA_F32], F32)
        arena_pos = {"o": 0}

        def carve(shape, dt):
            n = 1
            for d_ in shape:
                n *= d_
            words = (n * (2 if dt == BF16 else 4) + 3) // 4
            o = arena_pos["o"]
            arena_pos["o"] = o + words
            assert arena_pos["o"] <= ARENA_F32, arena_pos
            ap = ARENA[:, o:o + words]
            if dt == BF16:
                ap = ap.bitcast(BF16)[:, 0:n]
            if len(shape) == 2:
                ap = ap.rearrange("p (a b) -> p a b", b=shape[1])
            elif len(shape) == 3:
                ap = ap.rearrange("p (a b c) -> p a b c", b=shape[1], c=shape[2])
            return ap

        arena_pos["o"] = 0
        RB = [carve([NF, 128], BF16) for i in range(NB_)]
        HID = carve([NF, CH], BF16)
        SG = [carve([CH], F32) for i in range(2)]
        GSTMP = [carve([3968], BF16) for i in range(2)]
        R_gstmp = [Res(), Res()]
        SQ = [sb("sq%d" % i, [128, CH], BF16) for i in range(2)]
        RT = sb("rt", [128, CH], F32)
        RSTD = sb("rstd", [128, CH], F32)
        CST = sb("cstt", [128, NCST], F32)
        PW = sb("pw", [128, 2, D], BF16)
        PTB = sb("ptb", [128, 2, CH], BF16)
        onesb = sb("onesb", [128, 128], BF16)
        identb = sb("identb", [128, 128], BF16)
        PSALL = st.enter_context(nc.psum_tensor("psall", [128, 8, 512], F32))
        PS = [PSALL[:, i, :] for i in range(8)]

        PSR = [Res("ps%d" % i) for i in range(8)]
        Hres = [[Res() for dc in range(8)] for j in range(NCH)]
        Ures = [[Res() for dc in range(8)] for j in range(NCH)]
        Yres = [[Res() for k in range(4)] for j in range(NCH)]
        RAres = [Res() for _ in range(NA)]
        RBres = [Res() for _ in range(NB_)]
        HIDres = [Res() for _ in range(NF)]
        SGres = [Res(), Res()]
        SQres = [Res(), Res()]
        R_rt, R_rstd, R_cst, R_pw, R_ptb, R_ones, R_identb = [Res() for _ in range(7)]
        OUTres = [Res() for _ in range(NCH)]
        ring = {"a": 0, "b": 0}

        S.op("sp", lambda e: e.dma_start(out=CST[:], in_=cstd[:, :]), writes=[R_cst], dma=True)
        S.op("dve", lambda e: e.memset(onesb[:], 1.0), writes=[R_ones])
        S.op("dve", lambda e: e.tensor_copy(out=identb[:], in_=CST[:, C_ID:C_ID + 128]), reads=[R_cst], writes=[R_identb])

        gate_tok = {"r": []}

        def conv_group(dst, src_w, cols_list, kch, res):
            for i, c0 in cols_list:
                S.op("pool", lambda e, i=i, c0=c0: e.dma_start(
                    out=dst[i], in_=src_w[:, c0:c0 + 128].rearrange("(c p) n -> p c n", p=128)),
                    reads=gate_tok["r"], writes=[res], dma=True)

        def ffn_in_groups(dst, src_w, pre=None):
            groups = []
            for g0 in range(0, NF, 6):
                r = Res() if pre is None else pre[g0 // 6]
                lst = []
                for f in range(g0, min(NF, g0 + 6)):
                    lst.append((2 * f, f * 128))
                    lst.append((2 * f + 1, DFF + f * 128))
                conv_group(dst, src_w, lst, 8, r)
                groups.append(r)
            return [groups[f // 6] for f in range(NF)]

        R_f1i = ffn_in_groups(s_f1i, w1i)
        R_f1o = Res()
        conv_group(s_f1o, w1o, [(dm, dm * 128) for dm in range(8)], NF, R_f1o)
        R_win = {}
        R_gz, R_wa, R_wm, R_gate, R_wout = Res(), Res(), Res(), Res(), Res()
        for n in WIN_NAMES:
            R_win[n] = R_wa if n[0] == "a" else R_wm
        R_win["gate"] = R_gate
        R_f2i = [Res() for _ in range(4)]
        R_f2i = [R_f2i[f // 6] for f in range(NF)]
        R_f2o, R_wpg = Res(), Res()

        def conv_stage2():
            S.op("pool", lambda e: e.dma_start(out=s_win[WIN_IDX["gate"]], in_=zer[:, :].rearrange("p (c n) -> p c n", n=128)),
                 writes=[R_gz], dma=True)
            conv_group(s_win, win, [(WIN_IDX[n], WIN_SLABS[n]) for n in WIN_NAMES if n[0] == "a"], 8, R_wa)
            conv_group(s_win, win, [(WIN_IDX[n], WIN_SLABS[n]) for n in WIN_NAMES if n[0] == "m"], 8, R_wm)
            for (dcol, zcol) in ((0, 1536), (32, 1544), (64, 1540), (96, 1548)):
                S.op("pool", lambda e, dcol=dcol, zcol=zcol: e.dma_start(
                    out=s_win[WIN_IDX["gate"]][:, :, dcol:dcol + 4],
                    in_=win[:, zcol:zcol + 4].rearrange("(c p) n -> p c n", p=128)),
                    reads=[R_gz], writes=[R_gate], dma=True)
            for grp in range(2):
                for dm in range(8):
                    r0 = 0 if grp == 0 else 512
                    S.op("pool", lambda e, grp=grp, dm=dm, r0=r0: e.dma_start(
                        out=s_wout[grp, dm], in_=wout[r0:r0 + 512, dm * 128:(dm + 1) * 128].rearrange("(c p) n -> p c n", p=128)),
                        writes=[R_wout], dma=True)

        def conv_stage3():
            ffn_in_groups(s_f2i, w2i, [R_f2i[0], R_f2i[6], R_f2i[12], R_f2i[18]])
            conv_group(s_f2o, w2o, [(dm, dm * 128) for dm in range(8)], NF, R_f2o)
            conv_group(s_wpg, wpg, [(dm, dm * 128) for dm in range(8)], 8, R_wpg)
            S.op("pool", lambda e: e.dma_start(out=PW[:], in_=wpp[:, :].rearrange("(c p) n -> p c n", p=128)),
                 writes=[R_pw], dma=True)


        def loadA(src_ap, src_res, kch=8):
            slot = ring["a"] % NA
            ring["a"] += 1
            S.op("sp", lambda e: e.dma_start(out=RA[slot][:, 0:kch, :], in_=src_ap), reads=[src_res], writes=[RAres[slot]], dma=True)
            return RA[slot], RAres[slot]

        def loadB(src_ap, src_res):
            slot = ring["b"] % NB_
            ring["b"] += 1
            S.op("sp", lambda e: e.dma_start(out=RB[slot], in_=src_ap), reads=[src_res], writes=[RBres[slot]], dma=True)
            return RB[slot], RBres[slot]

        def mm(out, lhsT, rhs, start, stop, reads, writes, sgc=False):
            S.op("pe", lambda e: e.matmul(out, lhsT=lhsT, rhs=rhs, start=start, stop=stop, skip_group_check=sgc), reads=reads, writes=writes)

        def load_x(s, j):
            c0 = s * SEQ + j * CH
            S.op("sp", lambda e: e.dma_start(out=H[:, :, j * CH:(j + 1) * CH],
                                             in_=xT[:, c0:c0 + CH].rearrange("(c p) n -> p c n", p=128)),
                 writes=Hres[j], dma=True)

        def hap(j, dc):
            return H[:, dc, j * CH:(j + 1) * CH]

        def uap(j, dc):
            return U[:, dc, j * CH:(j + 1) * CH]

        def norm(j, gidx, dst_ap, dst_res):
            for dc in range(8):
                k = dc % 2
                S.op("act", lambda e, dc=dc, k=k: e.activation(out=SQ[k][:], in_=hap(j, dc), func=AF.Square),
                     reads=[Hres[j][dc]], writes=[SQres[k]])
                mm(PS[0], onesb[:], SQ[k][:], dc == 0, dc == 7, [SQres[k], R_ones], [PSR[0]])
            S.op("act", lambda e: e.activation(out=RT[:], in_=PS[0], func=AF.Sqrt, scale=1.0 / D, bias=EPS),
                 reads=[PSR[0]], writes=[R_rt])
            S.op("dve", lambda e: e.reciprocal(out=RSTD[:], in_=RT[:]), reads=[R_rt], writes=[R_rstd])
            for dc in range(8):
                S.op("dve", lambda e, dc=dc: e.scalar_tensor_tensor(
                    out=dst_ap(j, dc), in0=hap(j, dc), scalar=CST[:, C_G + gidx * 8 + dc:C_G + gidx * 8 + dc + 1],
                    in1=RSTD[:], op0=ALU.mult, op1=ALU.mult),
                    reads=[Hres[j][dc], R_rstd, R_cst], writes=[dst_res[j][dc]])
            return S.ops["dve"][-1]

        def ffn(j, s_in, R_in, s_o, R_o):
            for f in range(NF):
                gt, gr = loadA(s_in[2 * f], R_in[f])
                ut, ur = loadA(s_in[2 * f + 1], R_in[f])
                pg, pu = 1 + f % 2, 3 + f % 2
                for dc in range(8):
                    mm(PS[pg], gt[:, dc, :], uap(j, dc), dc == 0, dc == 7, [gr, Ures[j][dc]], [PSR[pg]])
                for dc in range(8):
                    mm(PS[pu], ut[:, dc, :], uap(j, dc), dc == 0, dc == 7, [ur, Ures[j][dc]], [PSR[pu]])
                k = f % 2
                S.op("act", lambda e, pg=pg, k=k: e.activation(out=SG[k][:], in_=PS[pg], func=AF.Silu),
                     reads=[PSR[pg]], writes=[SGres[k]])
                S.op("dve", lambda e, pu=pu, k=k, f=f: e.tensor_tensor(out=HID[:, f, :], in0=PS[pu], in1=SG[k][:], op=ALU.mult),
                     reads=[PSR[pu], SGres[k]], writes=[HIDres[f]])
            for dm in range(8):
                ot, orr = loadB(s_o[dm], R_o)
                po = 5 + dm % 2
                for fc in range(NF):
                    mm(PS[po], ot[:, fc, :], HID[:, fc, :], fc == 0, fc == NF - 1, [orr, HIDres[fc]], [PSR[po]])
                S.op("dve", lambda e, dm=dm, po=po: e.scalar_tensor_tensor(
                    out=hap(j, dm), in0=PS[po], scalar=0.5, in1=hap(j, dm), op0=ALU.mult, op1=ALU.add),
                    reads=[PSR[po], Hres[j][dm]], writes=[Hres[j][dm]])

        def phase_a(s, j):
            norm(j, 0, uap, Ures)
            ffn(j, s_f1i, R_f1i, s_f1o, R_f1o)
            return norm(j, 1, uap, Ures)

        def wout_proj(j, grp):
            for dm in range(8):
                wt, wr = loadA(s_wout[grp, dm], R_wout, 4)
                po = 5 + dm % 2
                for kc in range(4):
                    mm(PS[po], wt[:, kc, :], Y[:, kc, j * CH:(j + 1) * CH], kc == 0, kc == 3, [wr, Yres[j][kc]], [PSR[po]])
                S.op("dve", lambda e, dm=dm, po=po: e.tensor_tensor(out=hap(j, dm), in0=PS[po], in1=hap(j, dm), op=ALU.add),
                     reads=[PSR[po], Hres[j][dm]], writes=[Hres[j][dm]])

        def phase_c(s, j):
            norm(j, 2, uap, Ures)
            ffn(j, s_f2i, R_f2i, s_f2o, R_f2o)
            norm(j, 3, uap, Ures)
            c0 = s * SEQ + j * CH
            S.op("pool", lambda e: e.dma_start(out=PTB[:], in_=pT[:, c0:c0 + CH].rearrange("(c p) n -> p c n", p=128)),
                 writes=[R_ptb], dma=True)
            for dm in range(8):
                gt, gr = loadA(s_wpg[dm], R_wpg)
                pg, pu = 1 + dm % 2, 3 + dm % 2
                k = dm % 2
                for dc in range(8):
                    mm(PS[pg], gt[:, dc, :], uap(j, dc), dc == 0, dc == 7, [gr, Ures[j][dc]], [PSR[pg]])
                for pc in range(2):
                    mm(PS[pu], PW[:, pc, dm * 128:(dm + 1) * 128], PTB[:, pc, :], pc == 0, pc == 1, [R_pw, R_ptb], [PSR[pu]])
                S.op("act", lambda e, pg=pg, k=k: e.activation(out=SG[k][:], in_=PS[pg], func=AF.Sigmoid),
                     reads=[PSR[pg]], writes=[SGres[k]])
                S.op("dve", lambda e, pu=pu, k=k: e.tensor_tensor(out=SG[k][:], in0=PS[pu], in1=SG[k][:], op=ALU.mult),
                     reads=[PSR[pu], SGres[k]], writes=[SGres[k]])
                S.op("dve", lambda e, dm=dm, k=k: e.tensor_tensor(out=hap(j, dm), in0=hap(j, dm), in1=SG[k][:], op=ALU.add),
                     reads=[SGres[k], Hres[j][dm]], writes=[Hres[j][dm]])
            norm(j, 4, hap, Hres)
            return store(s, j)

        def store(s, j):
            c0 = s * SEQ + j * CH
            return S.op("pool", lambda e: e.dma_start(out=outT[:, c0:c0 + CH].rearrange("(c p) n -> p c n", p=128),
                                                      in_=H[:, :, j * CH:(j + 1) * CH]),
                        reads=Hres[j], writes=[OUTres[j]], dma=True)

        LAMT = sb("lamt", [128, 4], F32)
        LTMP = sb("ltmp", [128, 128], F32)
        R_lam, R_ltmp = Res(), Res()
        AX = mybir.AxisListType.X
        S.op("dve", lambda e: e.tensor_tensor(out=LTMP[:, 0:64], in0=CST[:, C_LAM:C_LAM + 64], in1=CST[:, C_LAM + 64:C_LAM + 128], op=ALU.mult),
             reads=[R_cst], writes=[R_ltmp])
        S.op("dve", lambda e: e.tensor_tensor(out=LTMP[:, 64:128], in0=CST[:, C_LAM + 128:C_LAM + 192], in1=CST[:, C_LAM + 192:C_LAM + 256], op=ALU.mult),
             reads=[R_cst], writes=[R_ltmp])
        S.op("dve", lambda e: e.reduce_sum(out=LAMT[:, 2:3], in_=LTMP[:, 0:64], axis=AX), reads=[R_ltmp], writes=[R_lam])
        S.op("dve", lambda e: e.reduce_sum(out=LAMT[:, 3:4], in_=LTMP[:, 64:128], axis=AX), reads=[R_ltmp], writes=[R_lam])
        S.op("act", lambda e: e.activation(out=LAMT[:, 2:4], in_=LAMT[:, 2:4], func=AF.Exp), reads=[R_lam], writes=[R_lam])
        S.op("dve", lambda e: e.tensor_tensor(out=LAMT[:, 0:1], in0=LAMT[:, 2:3], in1=LAMT[:, 3:4], op=ALU.subtract), reads=[R_lam], writes=[R_lam])
        S.op("dve", lambda e: e.tensor_scalar(out=LAMT[:, 0:1], in0=LAMT[:, 0:1], scalar1=LAM_INIT, scalar2=None, op0=ALU.add), reads=[R_lam], writes=[R_lam])
        S.op("dve", lambda e: e.tensor_scalar(out=LAMT[:, 1:2], in0=LAMT[:, 0:1], scalar1=-1.0, scalar2=None, op0=ALU.mult), reads=[R_lam], writes=[R_lam])

        arena_pos["o"] = 0
        GW = 3968
        QTs = [carve([SEQ], BF16) for _ in range(2)]
        KTs = [carve([SEQ], BF16) for _ in range(2)]
        VAs = [carve([16, 129], BF16) for _ in range(2)]
        GSs = [carve([GW], BF16) for _ in range(2)]
        EX = [carve([2, 512], BF16) for _ in range(3)]
        EG = [carve([2, 512], BF16) for _ in range(4)]
        ACCS = carve([8, 129], F32)
        OT = carve([4, 128], F32)
        T1 = carve([128], F32)
        YA4 = carve([4, 128], BF16)
        SQJ = carve([128], F32)
        REC = carve([16], F32)
        RS4 = carve([8], F32)
        ZR = carve([129], BF16)
        R_qts = [[Res() for _ in range(NCH)] for _ in range(2)]
        R_kts = [[Res() for _ in range(NCH)] for _ in range(2)]
        R_vas = [[Res() for _ in range(8)] for _ in range(2)]
        R_gss = [[Res() for _ in range(8)] for _ in range(2)]
        R_ex, R_eg = [Res(), Res(), Res()], [Res(), Res(), Res(), Res()]
        R_accs = Res()
        R_ot = [Res() for _ in range(4)]
        R_t1, R_ya4, R_sqj, R_rec, R_rs4, R_zr = Res(), Res(), Res(), Res(), Res(), Res()
        PSB7 = PSALL[:, 7, :].bitcast(BF16)
        _rb7 = Res()
        R_b7 = [_rb7, _rb7]

        def acc(a):
            return PSALL[:, 4 + a // 3, (a % 3) * 129:(a % 3) * 129 + 129]

        R_sgs = [Res() for _ in range(4)]

        arena_dma_reads = []

        def gen_strips():
            iob = Y[:, :, :].rearrange("p a b -> p (a b)").bitcast(F32)[:, 0:GW]
            yall = [r_ for l_ in Yres for r_ in l_]
            S.op("pool", lambda e: e.iota(iob, [[1, GW]], base=-1920, channel_multiplier=-1, allow_small_or_imprecise_dtypes=True), writes=yall)
            S.op("act", lambda e: e.activation(out=iob, in_=iob, func=AF.Abs), reads=yall, writes=yall)
            for h in range(4):
                slope = 2.0 ** (-2.0 * (h + 1))
                k = h % 2
                S.op("act", lambda e, k=k, slope=slope: e.activation(out=GSTMP[k], in_=iob, func=AF.Exp, scale=-slope), reads=yall, writes=[R_gstmp[k]])
                arena_dma_reads.append(S.op("sp", lambda e, h=h, k=k: e.dma_start(out=s_gs[h], in_=GSTMP[k]), reads=[R_gstmp[k]], writes=[R_sgs[h]], dma=True))

        def attention(s, dbg=False):
            half = {"n": 0}

            def next_half():
                hf = half["n"] % 2
                half["n"] += 1
                return hf

            for st_ in range(2):
                S.op("dve", lambda e, st_=st_: e.memset(VAs[st_][:, :, 128:129], 1.0), writes=R_vas[st_])
            S.op("dve", lambda e: e.memset(ZR, 0.0), writes=[R_zr])

            def prep_units(h, st_):
                QT, KT, VA, GS = QTs[st_], KTs[st_], VAs[st_], GSs[st_]
                slope = 2.0 ** (-2.0 * (h + 1))
                units = []
                slabs = {}

                def strip_load():
                    S.op("sp", lambda e: e.dma_start(out=GS, in_=s_gs[h]), reads=[R_sgs[h]], writes=R_gss[st_], dma=True)
                units.append(strip_load)

                def proj_qk(nm, dst, dres, j, hf):
                    if nm not in slabs:
                        slabs[nm] = loadA(s_win[WIN_IDX[nm]], R_win[nm])
                    wt, wr = slabs[nm]
                    bh = next_half()
                    c0 = j * CH + hf * 256
                    for dc in range(8):
                        mm(PSALL[:, 7, bh * 256:(bh + 1) * 256], wt[:, dc, :], U[:, dc, c0:c0 + 256], dc == 0, dc == 7, [wr, Ures[j][dc]], [R_b7[bh]])
                    S.op("dve", lambda e: e.tensor_copy(out=dst[:, c0:c0 + 256], in_=PSALL[:, 7, bh * 256:(bh + 1) * 256]),
                         reads=[R_b7[bh]], writes=[dres[j]])
                for j in range(NCH):
                    for hf in range(2):
                        units.append(lambda j=j, hf=hf: proj_qk("aq%d" % h, QT, R_qts[st_], j, hf))
                for j in range(NCH):
                    for hf in range(2):
                        units.append(lambda j=j, hf=hf: proj_qk("ak%d" % h, KT, R_kts[st_], j, hf))

                def proj_v(t2):
                    nm = "av%d" % h
                    if nm not in slabs:
                        slabs[nm] = loadA(s_win[WIN_IDX[nm]], R_win[nm])
                    wt, wr = slabs[nm]
                    bh = next_half()
                    for tq in range(2):
                        tt = t2 * 2 + tq
                        for dc in range(8):
                            mm(PSALL[:, 7, bh * 256 + tq * 128:bh * 256 + (tq + 1) * 128], U[:, dc, tt * 128:(tt + 1) * 128], wt[:, dc, :], dc == 0, dc == 7,
                               [wr, Ures[tt // 4][dc]], [R_b7[bh]])
                    S.op("dve", lambda e: e.tensor_copy(out=VA[:, t2 * 2:t2 * 2 + 2, 0:128],
                                                        in_=PSALL[:, 7, bh * 256:(bh + 1) * 256].rearrange("p (a b) -> p a b", b=128)),
                         reads=[R_b7[bh]], writes=[R_vas[st_][t2]])
                for t2 in range(8):
                    units.append(lambda t2=t2: proj_v(t2))
                return units

            def epilogue_rounds(qc, h):
                rounds = []
                r0 = [lambda: S.op("dve", lambda e: e.reciprocal(out=REC[:, 0:8].unsqueeze(2), in_=ACCS[:, :, 128:129]), reads=[R_accs], writes=[R_rec]),
                      lambda: S.op("dve", lambda e: e.tensor_scalar(out=REC[:, 8:12], in0=REC[:, 4:8], scalar1=LAMT[:, 1:2], scalar2=None, op0=ALU.mult),
                                   reads=[R_rec, R_lam], writes=[R_rec])]
                rounds.append(r0)
                for qb in range(4):
                    rounds.append([
                        lambda qb=qb: S.op("dve", lambda e: e.tensor_scalar(out=T1, in0=ACCS[:, 4 + qb, 0:128], scalar1=REC[:, 8 + qb:9 + qb], scalar2=None, op0=ALU.mult),
                                           reads=[R_accs, R_rec], writes=[R_t1]),
                        lambda qb=qb: S.op("dve", lambda e: e.scalar_tensor_tensor(out=OT[:, qb, :], in0=ACCS[:, qb, 0:128], scalar=REC[:, qb:qb + 1], in1=T1,
                                                                                  op0=ALU.mult, op1=ALU.add),
                                           reads=[R_accs, R_rec, R_t1], writes=[R_ot[qb]]),
                        lambda qb=qb: S.op("act", lambda e: e.activation(out=SQJ, in_=OT[:, qb, :], func=AF.Square, accum_out=REC[:, 12 + qb:13 + qb]),
                                           reads=[R_ot[qb]], writes=[R_sqj, R_rs4])])
                rounds.append([])
                rounds.append([lambda: S.op("act", lambda e: e.activation(out=RS4[:, 0:4], in_=REC[:, 12:16], func=AF.Sqrt, scale=1.0 / 128, bias=EPS),
                                            reads=[R_rs4], writes=[R_rs4])])
                rounds.append([lambda: S.op("dve", lambda e: e.reciprocal(out=RS4[:, 4:8], in_=RS4[:, 0:4]), reads=[R_rs4], writes=[R_rs4]),
                               lambda: S.op("dve", lambda e: e.tensor_scalar(out=RS4[:, 4:8], in0=RS4[:, 4:8], scalar1=1.0 - LAM_INIT, scalar2=None, op0=ALU.mult),
                                            reads=[R_rs4], writes=[R_rs4])])
                rounds.append([(lambda qb=qb: S.op("dve", lambda e: e.scalar_tensor_tensor(out=YA4[:, qb, :], in0=OT[:, qb, :], scalar=RS4[:, 4 + qb:5 + qb],
                                                                                          in1=CST[:, C_GA + h * 128:C_GA + (h + 1) * 128], op0=ALU.mult, op1=ALU.mult),
                                                   reads=[R_ot[qb], R_rs4, R_cst], writes=[R_ya4])) for qb in range(4)])
                rounds.append([])
                rounds.append([(lambda qb=qb: S.op("pe", lambda e: e.transpose(PSB7[:, 512 + qb * 128:512 + (qb + 1) * 128], YA4[:, qb, :], identb[:]),
                                                   reads=[R_ya4, R_identb], writes=[R_b7[1]])) for qb in range(4)])
                rounds.append([])
                rounds.append([lambda: S.op("act", lambda e: e.activation(out=Y[:, h, qc * CH:(qc + 1) * CH], in_=PSB7[:, 512:1024], func=AF.Copy),
                                            reads=[R_b7[1]], writes=[Yres[qc][h]])])
                return rounds

            for u in prep_units(0, 0):
                u()
            rounds = []
            for h in range(4):
                st_ = h % 2
                QT, KT, VA, GS = QTs[st_], KTs[st_], VAs[st_], GSs[st_]
                bg = prep_units(h + 1, 1 - st_) if h + 1 < 4 else []
                iters = [(qc, kb) for qc in range(NCH) for kb in range(16)]
                NIT = len(iters)
                LOOK = 3

                def qk(it):
                    qc, kb = iters[it]
                    sp = it % 2
                    for c in range(2):
                        mm(PSALL[:, 2 * sp + c, :], KT[c * 64:(c + 1) * 64, kb * 128:(kb + 1) * 128],
                           QT[c * 64:(c + 1) * 64, qc * CH:(qc + 1) * CH], True, True,
                           [R_kts[st_][kb // 4], R_qts[st_][qc]], [PSR[2 * sp + c]])
                    x2 = it % 3
                    x3 = it % 4
                    S.op("act", lambda e: e.activation(out=EX[x2], in_=PSALL[:, 2 * sp:2 * sp + 2, :], func=AF.Exp, scale=0.125),
                         reads=[PSR[2 * sp], PSR[2 * sp + 1]], writes=[R_ex[x2]])
                    off = 1920 + qc * CH - kb * 128
                    gres = [R_gss[st_][i] for i in range(off // 512, min(7, (off + 511) // 512) + 1)]
                    gwin = GS[:, off:off + 512].unsqueeze(1).to_broadcast([128, 2, 512])
                    S.op("dve", lambda e: e.tensor_tensor(out=EG[x3], in0=EX[x2], in1=gwin, op=ALU.mult),
                         reads=[R_ex[x2]] + gres, writes=[R_eg[x3]])

                def av(it):
                    qc, kb = iters[it]
                    x3 = it % 4
                    for c in range(2):
                        for qb in range(4):
                            a = c * 4 + qb
                            mm(acc(a), EG[x3][:, c, qb * 128:(qb + 1) * 128], VA[:, kb, :], kb == 0 and a % 3 == 0, kb == 15,
                               [R_eg[x3], R_vas[st_][kb // 2]], [PSR[4 + a // 3]], sgc=True)
                    if kb == 15:
                        S.op("dve", lambda e: e.tensor_copy(out=ACCS[:, 0:3, :], in_=PSALL[:, 4, 0:387].rearrange("p (a b) -> p a b", b=129)),
                             reads=[PSR[4]], writes=[R_accs])
                        S.op("dve", lambda e: e.tensor_copy(out=ACCS[:, 3:6, :], in_=PSALL[:, 5, 0:387].rearrange("p (a b) -> p a b", b=129)),
                             reads=[PSR[5]], writes=[R_accs])
                        S.op("dve", lambda e: e.tensor_copy(out=ACCS[:, 6:8, :], in_=PSALL[:, 6, 0:258].rearrange("p (a b) -> p a b", b=129)),
                             reads=[PSR[6]], writes=[R_accs])
                        rounds.extend(epilogue_rounds(qc, h))

                for t in range(NIT + LOOK):
                    if t < NIT:
                        qk(t)
                    if t >= LOOK:
                        av(t - LOOK)
                    if rounds:
                        for th in rounds.pop(0):
                            th()
                    if bg and t >= 4 and t % 2 == 0 and len(rounds) not in (1, 2, 3):
                        bg.pop(0)()
                while bg:
                    bg.pop(0)()
            while rounds:
                for th in rounds.pop(0):
                    th()

        arena_pos["o"] = 0
        BLK = [carve([2052], F32) for _ in range(3)]
        R_blk = [Res(), Res(), Res()]
        GA_, GB_, GC_ = BLK[0][:, 0:SEQ], BLK[1][:, 0:SEQ], BLK[2][:, 0:SEQ]
        PRE, ACCV = BLK[0][:, 0:2050], BLK[1][:, 0:SEQ]
        HACCs = [BLK[1][:, 0:SEQ].rearrange("p (a b) -> p a b", b=128), BLK[2][:, 0:SEQ].rearrange("p (a b) -> p a b", b=128)]
        TG = carve([16, 72], F32)
        DECB = carve([8, 16], F32)
        QTp = carve([SEQ], BF16)
        KTp = carve([SEQ], BF16)
        KTOK = carve([16, 128], BF16)
        VW = [carve([16, 129], BF16) for _ in range(2)]
        PT2 = [carve([2, 128], BF16) for _ in range(2)]
        CB3 = [[carve([129], BF16) for _ in range(3)] for _ in range(2)]
        MST = [carve([129], F32) for _ in range(2)]
        DEN = carve([8], F32)
        OGs = [carve([16, 128], BF16) for _ in range(2)]
        T2 = carve([128], F32)
        YM4 = carve([4, 128], BF16)
        SQM = carve([128], F32)
        RBT = carve([48], F32)
        SSHs = [carve([48], F32) for _ in range(2)]
        NBG = sb("nbg", [128, 2], F32)
        ONES1 = sb("ones1", [128, 1], F32)
        R_tg, R_decb, R_qtp, R_ktp = Res(), Res(), Res(), Res()
        R_ktok = [Res() for _ in range(16)]
        R_vw = [[Res() for _ in range(4)] for _ in range(2)]
        R_pt2, R_mst, R_den = [Res(), Res()], [Res(), Res()], [Res(), Res()]
        R_cb3 = [[Res() for _ in range(3)] for _ in range(2)]
        R_haccs = [[Res() for _ in range(16)] for _ in range(2)]
        R_ogs = [[Res() for _ in range(4)] for _ in range(2)]
        R_sshs = [Res(), Res()]
        R_og, R_t2, R_ym, R_sqm, R_rbt, R_ssh, R_nbg, R_ones1 = [Res() for _ in range(8)]
        PSB6 = PSALL[:, 6, :].bitcast(BF16)
        S.op("dve", lambda e: e.tensor_scalar(out=NBG[:, 0:1], in0=CST[:, C_BF:C_BF + 1], scalar1=-1.0, scalar2=None, op0=ALU.mult),
             reads=[R_cst], writes=[R_nbg])
        S.op("dve", lambda e: e.memset(ONES1[:], 1.0), writes=[R_ones1])

        def rev(ap):
            a = ap.ap
            assert len(a) == 2, a
            return bass.AP(ap.tensor, ap.offset + (a[1][1] - 1) * a[1][0], [[a[0][0], a[0][1]], [-a[1][0], a[1][1]]])

        def strided(ap, start, step, n):
            a = ap.ap
            assert len(a) == 2, a
            return bass.AP(ap.tensor, ap.offset + start * a[1][0], [[a[0][0], a[0][1]], [a[1][0] * step, n]])

        def mlstm(s, dbg=False):
            ident = CST[:, C_ID:C_ID + 128]
            pending = []
            gsl, gr = loadA(s_win[WIN_IDX["gate"]], R_win["gate"])
            for j in range(NCH):
                for dc in range(8):
                    mm(PSALL[0:36, 0, :], gsl[:, dc, 0:36], uap(j, dc), dc == 0, dc == 7, [gr, Ures[j][dc]], [PSR[0]])
                for dc in range(8):
                    mm(PSALL[0:36, 1, :], gsl[:, dc, 64:100], uap(j, dc), dc == 0, dc == 7, [gr, Ures[j][dc]], [PSR[1]])
                S.op("act", lambda e, j=j: e.activation(out=GA_[0:36, j * CH:(j + 1) * CH], in_=PSALL[0:36, 0, :], func=AF.Identity,
                                                        bias=CST[0:36, C_BI:C_BI + 1]),
                     reads=[PSR[0], R_cst], writes=[R_blk[0]])
                S.op("act", lambda e, j=j: e.activation(out=GB_[0:36, j * CH:(j + 1) * CH], in_=PSALL[0:36, 1, :], func=AF.Exp,
                                                        scale=-1.0, bias=NBG[0:36, 0:1]),
                     reads=[PSR[1], R_nbg], writes=[R_blk[1]])
            S.op("act", lambda e: e.activation(out=GB_[0:36, :], in_=GB_[0:36, :], func=AF.Ln, bias=1.0), reads=[R_blk[1]], writes=[R_blk[1]])
            if dbg:
                allr = [r_ for l_ in Hres for r_ in l_]
                S.op("dve", lambda e: e.tensor_copy(out=H[0:36, 6, :], in_=GA_[0:36, :]), reads=[R_blk[0]] + allr, writes=allr)
                S.op("dve", lambda e: e.tensor_copy(out=H[0:36, 7, :], in_=GB_[0:36, :]), reads=[R_blk[1]] + allr, writes=allr)
            S.op("dve", lambda e: e.memset(GC_[0:36, :], 0.0), writes=[R_blk[2]])
            S.op("dve", lambda e: e.tensor_tensor_scan(out=GC_[0:4, :], data0=ONES1[0:4, 0:1].to_broadcast([4, SEQ]), data1=GB_[0:4, :],
                                                       initial=0.0, op0=ALU.mult, op1=ALU.add),
                 reads=[R_blk[1], R_ones1], writes=[R_blk[2]])
            S.op("dve", lambda e: e.tensor_tensor_scan(out=rev(GC_[32:36, :]), data0=ONES1[32:36, 0:1].to_broadcast([4, SEQ]), data1=rev(GB_[32:36, :]),
                                                       initial=0.0, op0=ALU.mult, op1=ALU.add),
                 reads=[R_blk[1], R_ones1], writes=[R_blk[2]])
            S.op("dve", lambda e: e.tensor_tensor(out=GA_[0:36, :], in0=GA_[0:36, :], in1=GC_[0:36, :], op=ALU.add),
                 reads=[R_blk[0], R_blk[2]], writes=[R_blk[0]])
            S.op("dve", lambda e: e.tensor_tensor_scan(out=GB_[0:4, :], data0=ONES1[0:4, 0:1].to_broadcast([4, SEQ]), data1=GA_[0:4, :],
                                                       initial=0.0, op0=ALU.mult, op1=ALU.max),
                 reads=[R_blk[0], R_ones1], writes=[R_blk[1]])
            S.op("dve", lambda e: e.tensor_tensor_scan(out=rev(GB_[32:36, :]), data0=ONES1[32:36, 0:1].to_broadcast([4, SEQ]), data1=rev(GA_[32:36, :]),
                                                       initial=0.0, op0=ALU.mult, op1=ALU.max),
                 reads=[R_blk[0], R_ones1], writes=[R_blk[1]])
            S.op("dve", lambda e: e.memset(RBT[0:36, 0:32], 0.0), writes=[R_rbt])
            S.op("dve", lambda e: e.tensor_copy(out=RBT[0:4, 1:16], in_=strided(GB_[0:4, :], 127, 128, 15)), reads=[R_blk[1]], writes=[R_rbt])
            S.op("dve", lambda e: e.tensor_copy(out=RBT[32:36, 0:15], in_=strided(GB_[32:36, :], 128, 128, 15)), reads=[R_blk[1]], writes=[R_rbt])
            S.op("dve", lambda e: e.tensor_copy(out=RBT[0:4, 16:32], in_=strided(GB_[0:4, :], 127, 128, 16)), reads=[R_blk[1]], writes=[R_rbt])
            S.op("dve", lambda e: e.tensor_copy(out=RBT[32:36, 16:32], in_=strided(GB_[32:36, :], 0, 128, 16)), reads=[R_blk[1]], writes=[R_rbt])
            S.op("dve", lambda e: e.tensor_tensor(out=RBT[0:36, 32:48], in0=RBT[0:36, 0:16], in1=RBT[0:36, 16:32], op=ALU.subtract),
                 reads=[R_rbt], writes=[R_rbt])
            S.op("act", lambda e: e.activation(out=RBT[0:36, 32:48], in_=RBT[0:36, 32:48], func=AF.Exp), reads=[R_rbt], writes=[R_rbt])
            for (buf, rb) in ((GA_, R_blk[0]), (GC_, R_blk[2])):
                v3 = buf[0:36, :].rearrange("p (a b) -> p a b", b=128)
                S.op("dve", lambda e, v3=v3: e.tensor_tensor(out=v3, in0=v3, in1=RBT[0:36, 0:16].unsqueeze(2).to_broadcast([36, 16, 128]), op=ALU.subtract),
                     reads=[rb, R_rbt], writes=[rb])
                S.op("act", lambda e, buf=buf: e.activation(out=buf[0:36, :], in_=buf[0:36, :], func=AF.Exp), reads=[rb], writes=[rb])
            for t4 in range(4):
                for tq in range(4):
                    tt = t4 * 4 + tq
                    S.op("pe", lambda e, tt=tt, tq=tq: e.transpose(PSALL[:, 2, tq * 72:tq * 72 + 36], GA_[0:36, tt * 128:(tt + 1) * 128], ident[0:36, 0:36]),
                         reads=[R_blk[0], R_cst], writes=[PSR[2]])
                    S.op("pe", lambda e, tt=tt, tq=tq: e.transpose(PSALL[:, 2, tq * 72 + 36:tq * 72 + 72], GC_[0:36, tt * 128:(tt + 1) * 128], ident[0:36, 0:36]),
                         reads=[R_blk[2], R_cst], writes=[PSR[2]])
                S.op("act", lambda e, t4=t4: e.activation(out=TG[:, t4 * 4:(t4 + 1) * 4, :], in_=PSALL[:, 2, 0:288].rearrange("p (a b) -> p a b", b=72), func=AF.Copy),
                     reads=[PSR[2]], writes=[R_tg])
            for dr in range(2):
                for h in range(4):
                    ch = dr * 4 + h
                    mm(PSALL[:, 3, ch * 16:(ch + 1) * 16], CST[dr * 32:dr * 32 + 4, C_SEL + h * 128:C_SEL + (h + 1) * 128],
                       RBT[dr * 32:dr * 32 + 4, 32:48], True, True, [R_cst, R_rbt], [PSR[3]])
            S.op("dve", lambda e: e.tensor_copy(out=DECB, in_=PSALL[:, 3, 0:128].rearrange("p (a b) -> p a b", b=16)), reads=[PSR[3]], writes=[R_decb])

            for g in range(2):
                for (nm, dstT, dres, ccol, isk) in (("mq%d" % g, QTp, R_qtp, C_CQ + 3 * g, False), ("mk%d" % g, KTp, R_ktp, C_CK + 3 * g, True)):
                    wt, wr = loadA(s_win[WIN_IDX[nm]], R_win[nm])
                    S.op("dve", lambda e: e.memset(PRE[:, 0:1], 0.0), writes=[R_blk[0]])
                    S.op("dve", lambda e: e.memset(PRE[:, 2049:2050], 0.0), writes=[R_blk[0]])
                    for j in range(NCH):
                        b = 6 + j % 2
                        for dc in range(8):
                            mm(PS[b], wt[:, dc, :], uap(j, dc), dc == 0, dc == 7, [wr, Ures[j][dc]], [PSR[b]])
                        S.op("act", lambda e, j=j, b=b: e.activation(out=PRE[:, 1 + j * CH:1 + (j + 1) * CH], in_=PS[b], func=AF.Copy),
                             reads=[PSR[b]], writes=[R_blk[0]])
                    S.op("dve", lambda e, ccol=ccol: e.tensor_scalar(out=ACCV, in0=PRE[:, 1:2049], scalar1=CST[:, ccol + 1:ccol + 2], scalar2=None, op0=ALU.mult),
                         reads=[R_blk[0], R_cst], writes=[R_blk[1]])
                    S.op("dve", lambda e, ccol=ccol: e.scalar_tensor_tensor(out=ACCV, in0=PRE[:, 0:2048], scalar=CST[:, ccol:ccol + 1], in1=ACCV, op0=ALU.mult, op1=ALU.add),
                         reads=[R_blk[0], R_blk[1], R_cst], writes=[R_blk[1]])
                    S.op("dve", lambda e, ccol=ccol: e.scalar_tensor_tensor(out=ACCV, in0=PRE[:, 2:2050], scalar=CST[:, ccol + 2:ccol + 3], in1=ACCV, op0=ALU.mult, op1=ALU.add),
                         reads=[R_blk[0], R_blk[1], R_cst], writes=[R_blk[1]])
                    if not isk:
                        S.op("act", lambda e, dstT=dstT: e.activation(out=dstT, in_=ACCV, func=AF.Silu), reads=[R_blk[1]], writes=[dres])
                    else:
                        S.op("act", lambda e: e.activation(out=ACCV, in_=ACCV, func=AF.Silu), reads=[R_blk[1]], writes=[R_blk[1]])
                        S.op("dve", lambda e, dstT=dstT: e.tensor_scalar(out=dstT, in0=ACCV, scalar1=0.125, scalar2=None, op0=ALU.mult),
                             reads=[R_blk[1]], writes=[dres])
                for tt in range(16):
                    S.op("pe", lambda e, tt=tt: e.transpose(PSB6[:, (tt % 4) * 128:(tt % 4 + 1) * 128], KTp[:, tt * 128:(tt + 1) * 128], identb[:]),
                         reads=[R_ktp, R_identb], writes=[PSR[6]])
                    if tt % 4 == 3:
                        t4 = tt // 4
                        S.op("dve", lambda e, t4=t4: e.tensor_copy(out=KTOK[:, t4 * 4:(t4 + 1) * 4, :], in_=PSB6[:, 0:512].rearrange("p (a b) -> p a b", b=128)),
                             reads=[PSR[6]], writes=R_ktok[t4 * 4:(t4 + 1) * 4])
                for hh in range(2):
                    h = 2 * g + hh
                    P0 = hh * 64
                    vt, vr = loadA(s_win[WIN_IDX["mv%d" % h]], R_win["mv%d" % h])
                    for t4 in range(4):
                        for tq in range(4):
                            tt = t4 * 4 + tq
                            for dc in range(8):
                                mm(PSALL[:, 7, tq * 128:(tq + 1) * 128], U[:, dc, tt * 128:(tt + 1) * 128], vt[:, dc, :], dc == 0, dc == 7,
                                   [vr, Ures[t4][dc]], [PSR[7]])
                        for dr in range(2):
                            col = dr * 32 + h
                            S.op("dve", lambda e, t4=t4, dr=dr, col=col: e.tensor_tensor(
                                out=VW[dr][:, t4 * 4:(t4 + 1) * 4, 0:128], in0=PS[7].rearrange("p (a b) -> p a b", b=128),
                                in1=TG[:, t4 * 4:(t4 + 1) * 4, col:col + 1].to_broadcast([128, 4, 128]), op=ALU.mult),
                                reads=[PSR[7], R_tg], writes=[R_vw[dr][t4]])
                    for dr in range(2):
                        col = dr * 32 + h
                        S.op("act", lambda e, dr=dr, col=col: e.activation(out=VW[dr][:, :, 128:129], in_=TG[:, :, col:col + 1], func=AF.Copy),
                             reads=[R_tg], writes=R_vw[dr])
                    hb = h % 2
                    while pending and pending[0][0] <= h - 2:
                        for th in pending.pop(0)[1]:
                            th()
                    HACC = HACCs[hb]
                    R_hacc = R_haccs[hb]
                    rblk = R_blk[1 + hb]
                    S.op("dve", lambda e, hb=hb: e.memset(BLK[1 + hb][:, 0:SEQ], 0.0), writes=[rblk] + R_hacc)
                    def tt_of(i, dr):
                        return i if dr == 0 else 15 - i

                    def sk_pe(i, h=h, P0=P0):
                        par = i % 2
                        for dr in range(2):
                            tt = tt_of(i, dr)
                            tc_ = slice(tt * 128, (tt + 1) * 128)
                            mm(PSALL[:, par, dr * 128:(dr + 1) * 128], KTp[P0:P0 + 64, tc_], QTp[P0:P0 + 64, tc_], True, True, [R_ktp, R_qtp], [PSR[par]])
                        for dr in range(2):
                            tt = tt_of(i, dr)
                            mm(PSALL[:, 4 + par, dr * 129:(dr + 1) * 129], KTOK[:, tt, :], VW[dr][:, tt, :], True, True,
                               [R_ktok[tt], R_vw[dr][tt // 4]], [PSR[4 + par]])

                    def sk_dve(i, h=h, P0=P0):
                        par = i % 2
                        S.op("dve", lambda e, par=par: e.tensor_tensor(out=PT2[par], in0=PSALL[:, par, 0:256].rearrange("p (a b) -> p a b", b=128),
                                                                     in1=CST[:, C_MU:C_MU + 256].rearrange("p (a b) -> p a b", b=128), op=ALU.mult),
                             reads=[PSR[par], R_cst], writes=[R_pt2[par]])
                        for dr in range(2):
                            ch = dr * 4 + h
                            kv = PSALL[P0:P0 + 64, 4 + par, dr * 129:(dr + 1) * 129]
                            if i == 0:
                                S.op("dve", lambda e, dr=dr, kv=kv: e.tensor_copy(out=MST[dr][P0:P0 + 64, :], in_=kv),
                                     reads=[PSR[4 + par]], writes=[R_mst[dr]])
                            else:
                                pt_ = tt_of(i - 1, dr)
                                S.op("dve", lambda e, dr=dr, kv=kv, ch=ch, pt_=pt_: e.scalar_tensor_tensor(
                                    out=MST[dr][P0:P0 + 64, :], in0=MST[dr][P0:P0 + 64, :], scalar=DECB[P0:P0 + 64, ch, pt_:pt_ + 1],
                                    in1=kv, op0=ALU.mult, op1=ALU.add),
                                    reads=[PSR[4 + par], R_mst[dr], R_decb], writes=[R_mst[dr]])

                    def sk_cb(i, h=h, P0=P0):
                        if i >= 15:
                            return
                        g3 = i % 3
                        for dr in range(2):
                            tt = tt_of(i, dr)
                            ch = dr * 4 + h
                            S.op("act", lambda e, dr=dr, ch=ch, tt=tt, g3=g3: e.activation(out=CB3[dr][g3][P0:P0 + 64, :], in_=MST[dr][P0:P0 + 64, :], func=AF.Copy,
                                                                                     scale=DECB[P0:P0 + 64, ch, tt:tt + 1]),
                                 reads=[R_mst[dr], R_decb], writes=[R_cb3[dr][g3]])

                    def a_pe(i, h=h, P0=P0):
                        par = i % 2
                        first = i == 0
                        for dr in range(2):
                            tt = tt_of(i, dr)
                            tc_ = slice(tt * 128, (tt + 1) * 128)
                            oa = PSALL[:, 2 + par, dr * 129:(dr + 1) * 129]
                            mm(oa, PT2[par][:, dr, :], VW[dr][:, tt, :], True, first, [R_pt2[par], R_vw[dr][tt // 4]], [PSR[2 + par]])
                            if not first:
                                g3 = (i - 1) % 3
                                mm(oa, QTp[P0:P0 + 64, tc_], CB3[dr][g3][P0:P0 + 64, :], False, True, [R_qtp, R_cb3[dr][g3]], [PSR[2 + par]])
                        S.op("act", lambda e, par=par: e.activation(out=DEN[:, 0:2].unsqueeze(2),
                                                                    in_=PSALL[:, 2 + par, 0:258].rearrange("p (a b) -> p a b", b=129)[:, :, 128:129], func=AF.Abs),
                             reads=[PSR[2 + par]], writes=[R_den[0]])

                    def a_out(i, h=h, P0=P0, HACC=HACC, R_hacc=R_hacc, rblk=rblk):
                        par = i % 2
                        for dr in range(2):
                            tt = tt_of(i, dr)
                            clc = 36 + dr * 32 + h
                            S.op("dve", lambda e, dr=dr, tt=tt, clc=clc: e.tensor_tensor(out=DEN[:, 2 + dr:3 + dr], in0=DEN[:, dr:dr + 1], in1=TG[:, tt, clc:clc + 1], op=ALU.max),
                                 reads=[R_den[0], R_tg], writes=[R_den[0]])
                        S.op("dve", lambda e: e.reciprocal(out=DEN[:, 4:6], in_=DEN[:, 2:4]), reads=[R_den[0]], writes=[R_den[0]])
                        for dr in range(2):
                            tt = tt_of(i, dr)
                            S.op("dve", lambda e, dr=dr, tt=tt, par=par: e.scalar_tensor_tensor(
                                out=HACC[:, tt, :], in0=PSALL[:, 2 + par, dr * 129:dr * 129 + 128], scalar=DEN[:, 4 + dr:5 + dr], in1=HACC[:, tt, :], op0=ALU.mult, op1=ALU.add),
                                reads=[PSR[2 + par], R_den[0], R_hacc[tt], rblk], writes=[R_hacc[tt]])


                    OG, R_og4, SSH, R_ssh = OGs[hb], R_ogs[hb], SSHs[hb], R_sshs[hb]
                    oslab = {}

                    def oproj(t4, h=h, OG=OG, R_og4=R_og4):
                        if "s" not in oslab:
                            oslab["s"] = loadA(s_win[WIN_IDX["mo%d" % h]], R_win["mo%d" % h])
                        ot, orr = oslab["s"]
                        for tq in range(4):
                            tt = t4 * 4 + tq
                            for dc in range(8):
                                mm(PSALL[:, 6, tq * 128:(tq + 1) * 128], U[:, dc, tt * 128:(tt + 1) * 128], ot[:, dc, :], dc == 0, dc == 7,
                                   [orr, Ures[t4][dc]], [PSR[6]])
                        S.op("act", lambda e: e.activation(out=OG[:, t4 * 4:(t4 + 1) * 4, :], in_=PS[6].rearrange("p (a b) -> p a b", b=128), func=AF.Sigmoid),
                             reads=[PSR[6]], writes=[R_og4[t4]])
                    bg = [(lambda t4=t4: oproj(t4)) for t4 in range(4)]

                    def ssq(tt, HACC=HACC, R_hacc=R_hacc, rblk=rblk, SSH=SSH, R_ssh=R_ssh):
                        S.op("act", lambda e: e.activation(out=SQM, in_=HACC[:, tt, :], func=AF.Square, accum_out=SSH[:, tt:tt + 1]),
                             reads=[R_hacc[tt], rblk], writes=[R_sqm, R_ssh])

                    def pop_pending(only_dve):
                        if not pending:
                            return False
                        if only_dve and pending[0][2] != "dve":
                            return False
                        for th in pending.pop(0)[1]:
                            th()
                        return True

                    sk_pe(0)
                    sk_dve(0)
                    sk_cb(0)
                    for i in range(16):
                        if i + 1 < 16:
                            sk_pe(i + 1)
                        a_pe(i)
                        if i + 1 < 16:
                            sk_dve(i + 1)
                            sk_cb(i + 1)
                        pop_pending(True)
                        a_out(i)
                        if i >= 8:
                            ssq(i)
                            ssq(15 - i)
                        if bg:
                            bg.pop(0)()
                        else:
                            pop_pending(False)
                    while bg:
                        bg.pop(0)()
                    S.op("act", lambda e, SSH=SSH: e.activation(out=SSH[:, 16:32], in_=SSH[:, 0:16], func=AF.Sqrt, scale=1.0 / 128, bias=EPS), reads=[R_ssh], writes=[R_ssh])
                    S.op("dve", lambda e, SSH=SSH: e.reciprocal(out=SSH[:, 32:48], in_=SSH[:, 16:32]), reads=[R_ssh], writes=[R_ssh])

                    def final_rounds(h=h, HACC=HACC, R_hacc=R_hacc, rblk=rblk, OG=OG, R_og4=R_og4, SSH=SSH, R_ssh=R_ssh):
                        rounds = []
                        for t4 in range(4):
                            for tq in range(4):
                                tt = t4 * 4 + tq
                                rounds.append(("dve", [
                                    lambda tt=tt: S.op("dve", lambda e: e.scalar_tensor_tensor(out=T2, in0=HACC[:, tt, :], scalar=SSH[:, 32 + tt:33 + tt],
                                                                                              in1=CST[:, C_GM + h * 128:C_GM + (h + 1) * 128], op0=ALU.mult, op1=ALU.mult),
                                                       reads=[R_hacc[tt], rblk, R_ssh, R_cst], writes=[R_t2]),
                                    lambda tt=tt, tq=tq, t4=t4: S.op("dve", lambda e: e.tensor_tensor(out=YM4[:, tq, :], in0=T2, in1=OG[:, tt, :], op=ALU.mult),
                                                                      reads=[R_t2, R_og4[t4]], writes=[R_ym])]))
                            rounds.append(("pe", [(lambda tq=tq: S.op("pe", lambda e: e.transpose(PSB7[:, tq * 128:(tq + 1) * 128], YM4[:, tq, :], identb[:]),
                                                                      reads=[R_ym, R_identb], writes=[PSR[7]])) for tq in range(4)]))
                            rounds.append(("act", [lambda t4=t4: S.op("act", lambda e: e.activation(out=Y[:, h, t4 * CH:(t4 + 1) * CH], in_=PSB7[:, 0:512], func=AF.Copy),
                                                                      reads=[PSR[7]], writes=[Yres[t4][h]])]))
                        return rounds
                    pending.extend([(h, r_[1], r_[0]) for r_ in final_rounds()])
            while pending:
                for th in pending.pop(0)[1]:
                    th()


        finals = []
        nseq = 1 if stage in ("att", "mlstm", "attdbg", "mldbg") else 2
        for s in range(nseq):
            for j in range(NCH):
                load_x(s, j)
                last = phase_a(s, j)
                if s == 0 and j == 0:
                    tok = Res()
                    tok.last_w = last
                    gate_tok["r"] = [tok]
                    conv_stage2()
                    gate_tok["r"] = []
                if s == 0 and j == 2:
                    gen_strips()
                    tok = Res()
                    tok.last_w = last
                    gate_tok["r"] = [tok]
                    conv_stage3()
                    gate_tok["r"] = []
                if stage == "A":
                    finals.append(store(s, j))
            if stage == "A":
                continue
            if stage in ("mlstm", "mldbg"):
                S.barrier(arena_dma_reads)
                mlstm(s, dbg=(stage == "mldbg"))
                for j in range(NCH):
                    if stage == "mlstm":
                        for hh in range(4):
                            S.op("dve", lambda e, j=j, hh=hh: e.tensor_copy(out=hap(j, hh), in_=Y[:, hh, j * CH:(j + 1) * CH]),
                                 reads=[Yres[j][hh], Hres[j][hh]], writes=[Hres[j][hh]])
                    finals.append(store(s, j))
                continue
            if stage != "AC":
                S.barrier(arena_dma_reads)
                attention(s)
                if stage == "attdbg":
                    for j in range(NCH):
                        finals.append(store(s, j))
                    continue
                if stage == "att":
                    for j in range(NCH):
                        c0 = j * CH
                        for hh in range(4):
                            S.op("dve", lambda e, j=j, hh=hh: e.tensor_copy(out=hap(j, hh), in_=Y[:, hh, j * CH:(j + 1) * CH]),
                                 reads=[Yres[j][hh], Hres[j][hh]], writes=[Hres[j][hh]])
                        finals.append(store(s, j))
                    continue
                S.barrier(arena_dma_reads)
                for j in range(NCH):
                    wout_proj(j, 1)
                S.barrier(arena_dma_reads)
                mlstm(s)
                S.barrier(arena_dma_reads)
                for j in range(NCH):
                    wout_proj(j, 0)
                S.barrier(arena_dma_reads)
            for j in range(NCH):
                finals.append(phase_c(s, j))
        print("sched stats", S.stats(), "dsems", S.ndsem)
        S.emit_all(final_ops=finals)
    return nc


def kernel(**inp):
    return run_kernel(inp, "full")


def run_kernel(inp, stage="full", ncores=NCORES):
    x = np.asarray(inp["x"], np.float32)
    p = np.asarray(inp["p"], np.float32)[0]
    cst = pack_consts({k: np.asarray(v, np.float32) for k, v in inp.items() if k not in ("x", "p")})
    zer = np.zeros((128, 1024), np.float32)
    shared = {
        "w1i": np.ascontiguousarray(inp["w_ffn1_in"][0], np.float32), "w1o": np.ascontiguousarray(inp["w_ffn1_out"][0], np.float32),
        "w2i": np.ascontiguousarray(inp["w_ffn2_in"][0], np.float32), "w2o": np.ascontiguousarray(inp["w_ffn2_out"][0], np.float32),
        "win": np.ascontiguousarray(inp["w_in"][0], np.float32), "wout": np.ascontiguousarray(inp["w_out"][0], np.float32),
        "wpg": np.ascontiguousarray(inp["w_ple_gate"][0], np.float32), "wpp": np.ascontiguousarray(inp["w_ple_proj"][0], np.float32),
        "cst": cst, "zer": zer,
    }
    in_maps = []
    for c in range(ncores):
        xs = x[2 * c:2 * c + 2].reshape(TOK, D)
        ps = p[2 * c:2 * c + 2].reshape(TOK, 256)
        m = dict(shared)
        m["xT"] = np.ascontiguousarray(xs.T)
        m["pT"] = np.ascontiguousarray(ps.T)
        in_maps.append(m)
    nc = build_nc(stage)
    res = run_bass_kernel_spmd(nc, in_maps, core_ids=list(range(ncores)))
    out = np.empty((2 * ncores, SEQ, D), np.float32)
    for c in range(ncores):
        out[2 * c:2 * c + 2] = np.ascontiguousarray(res.results[c]["outT"].T).reshape(2, SEQ, D)
    return out
```
